# Optimizing a Trainium2 kernel written in Bass

```python
import math
import jax, jax.numpy as jnp
from jax import lax
import numpy as np

D_MODEL = 1024
BATCH = 4
SEQ = 4096
DEPTH = 2

MLA_HEADS = 8
MLA_Q_LORA = 256
MLA_KV_LORA = 128
MLA_NOPE = 64
MLA_ROPE = 32
MLA_V = 64
MLA_WIDTH = MLA_HEADS * MLA_V
DIFF_HEADS = 4
DIFF_HEAD_DIM = 64
DIFF_WIDTH = DIFF_HEADS * 2 * DIFF_HEAD_DIM
DIFF_ROT = DIFF_HEAD_DIM // 4
ROPE_THETA = 500000.0
Q_BLOCK = 128
DEEPNORM_ALPHA = (2 * DEPTH) ** 0.25
DEEPNORM_BETA = (8 * DEPTH) ** -0.25
LN_EPS = 1e-5
RMS_EPS = 1e-6
IN_SPLITS = (MLA_Q_LORA, MLA_KV_LORA, MLA_ROPE, MLA_WIDTH,
             DIFF_WIDTH, DIFF_WIDTH, DIFF_WIDTH, DIFF_WIDTH, 2 * D_MODEL)
IN_COLS = sum(IN_SPLITS)

kernel_name = "hybrid_mla_diffattn_gated_deepnorm"


def _rms_norm(x, g, eps=RMS_EPS):
    xf = x.astype(jnp.float32)
    y = xf * lax.rsqrt(jnp.mean(xf * xf, axis=-1, keepdims=True) + eps)
    return (y * g.astype(jnp.float32)).astype(x.dtype)


def _layer_norm(x, g, b, eps=LN_EPS):
    xf = x.astype(jnp.float32)
    mu = jnp.mean(xf, axis=-1, keepdims=True)
    var = jnp.mean(jnp.square(xf - mu), axis=-1, keepdims=True)
    y = (xf - mu) * lax.rsqrt(var + eps)
    return (y * g.astype(jnp.float32) + b.astype(jnp.float32)).astype(x.dtype)


def _rotary(x, rot_dim):
    seq = x.shape[1]
    half = rot_dim // 2
    inv_freq = ROPE_THETA ** (-jnp.arange(half, dtype=jnp.float32) / half)
    ang = jnp.arange(seq, dtype=jnp.float32)[:, None] * inv_freq[None, :]
    shape = (seq,) + (1,) * (x.ndim - 3) + (half,)
    cos = jnp.cos(ang).reshape(shape)
    sin = jnp.sin(ang).reshape(shape)
    xr = x[..., :rot_dim].astype(jnp.float32)
    x1, x2 = xr[..., :half], xr[..., half:]
    rot = jnp.concatenate([x1 * cos - x2 * sin, x2 * cos + x1 * sin], axis=-1).astype(x.dtype)
    return jnp.concatenate([rot, x[..., rot_dim:]], axis=-1)


def _split_cols(h):
    parts, start = [], 0
    for size in IN_SPLITS:
        parts.append(h[..., start:start + size])
        start += size
    return parts


def _dense_attention(q, k, v, scale):
    b, h, s, dk = q.shape
    nb = s // Q_BLOCK
    qb = q.reshape(b, h, nb, Q_BLOCK, dk).transpose(2, 0, 1, 3, 4)

    def one_block(q_blk):
        sc = jnp.einsum('bhqd,bhkd->bhqk', q_blk, k, preferred_element_type=jnp.float32) * scale
        p = jax.nn.softmax(sc, axis=-1).astype(v.dtype)
        return jnp.einsum('bhqk,bhkd->bhqd', p, v)

    o = lax.map(one_block, qb)
    return o.transpose(1, 2, 0, 3, 4).reshape(b, h, s, v.shape[-1])


def _differential_attention(q, k, v, lam, scale):
    b, h, two, s, d = q.shape
    nb = s // Q_BLOCK
    qb = q.reshape(b, h, two, nb, Q_BLOCK, d).transpose(3, 0, 1, 2, 4, 5)

    def one_block(q_blk):
        sc = jnp.einsum('bhmqd,bhmkd->bhmqk', q_blk, k, preferred_element_type=jnp.float32) * scale
        p = jax.nn.softmax(sc, axis=-1)
        p_diff = (p[:, :, 0] - lam * p[:, :, 1]).astype(v.dtype)
        return jnp.einsum('bhqk,bhkd->bhqd', p_diff, v)

    o = lax.map(one_block, qb)
    return o.transpose(1, 2, 0, 3, 4).reshape(b, h, s, v.shape[-1])


def setup_inputs(seed: int = 0) -> dict:
    key = jax.random.key(seed)
    ks = jax.random.split(key, 16)
    f32 = jnp.float32

    def nrm(k, shape, scale):
        return jax.random.normal(k, shape, f32) * scale

    return {
        "x": jax.random.normal(ks[0], (BATCH, SEQ, D_MODEL), f32),
        "w_in": nrm(ks[1], (DEPTH, D_MODEL, IN_COLS), D_MODEL ** -0.5),
        "g_q": 1.0 + nrm(ks[2], (DEPTH, MLA_Q_LORA), 0.02),
        "w_q_up": nrm(ks[3], (DEPTH, MLA_Q_LORA, MLA_HEADS * (MLA_NOPE + MLA_ROPE)), MLA_Q_LORA ** -0.5),
        "g_kv": 1.0 + nrm(ks[4], (DEPTH, MLA_KV_LORA), 0.02),
        "w_kv_up": nrm(ks[5], (DEPTH, MLA_KV_LORA, MLA_HEADS * (MLA_NOPE + MLA_V)), MLA_KV_LORA ** -0.5),
        "diff_lambda": nrm(ks[6], (DEPTH, 4, DIFF_HEAD_DIM), 0.1),
        "g_diff": 1.0 + nrm(ks[7], (DEPTH, 2 * DIFF_HEAD_DIM), 0.02),
        "w_branch_a": nrm(ks[8], (DEPTH, MLA_WIDTH, D_MODEL), MLA_WIDTH ** -0.5),
        "w_branch_b": nrm(ks[9], (DEPTH, DIFF_WIDTH, D_MODEL), DIFF_WIDTH ** -0.5),
        "b_merge": nrm(ks[10], (DEPTH, 2 * D_MODEL), 0.02),
        "w_out": nrm(ks[11], (DEPTH, D_MODEL, D_MODEL), D_MODEL ** -0.5 * DEEPNORM_BETA),
        "ln_gamma": 1.0 + nrm(ks[12], (DEPTH, D_MODEL), 0.02),
        "ln_beta": nrm(ks[13], (DEPTH, D_MODEL), 0.02),
    }


def reference(x, w_in, g_q, w_q_up, g_kv, w_kv_up, diff_lambda, g_diff,
              w_branch_a, w_branch_b, b_merge, w_out, ln_gamma, ln_beta):
    b, s, _ = x.shape
    mla_scale = 1.0 / math.sqrt(MLA_NOPE + MLA_ROPE)
    diff_scale = 1.0 / math.sqrt(DIFF_HEAD_DIM)
    for l in range(DEPTH):
        h = x @ w_in[l]
        c_q, c_kv, k_r, gate_a, q_d, k_d, v_d, gate_b, gates = _split_cols(h)

        q_a = (_rms_norm(c_q, g_q[l]) @ w_q_up[l]).reshape(b, s, MLA_HEADS, MLA_NOPE + MLA_ROPE)
        q_a = jnp.concatenate([q_a[..., :MLA_NOPE], _rotary(q_a[..., MLA_NOPE:], MLA_ROPE)], axis=-1)
        kv = (_rms_norm(c_kv, g_kv[l]) @ w_kv_up[l]).reshape(b, s, MLA_HEADS, MLA_NOPE + MLA_V)
        k_rope = _rotary(k_r, MLA_ROPE)
        k_a = jnp.concatenate(
            [kv[..., :MLA_NOPE], jnp.broadcast_to(k_rope[:, :, None, :], (b, s, MLA_HEADS, MLA_ROPE))], axis=-1)
        v_a = kv[..., MLA_NOPE:]
        o_a = _dense_attention(q_a.transpose(0, 2, 1, 3), k_a.transpose(0, 2, 1, 3),
                               v_a.transpose(0, 2, 1, 3), mla_scale)
        y_a = o_a.transpose(0, 2, 1, 3).reshape(b, s, MLA_WIDTH) * jax.nn.silu(gate_a)
        y_a = y_a @ w_branch_a[l]

        q_b = _rotary(q_d.reshape(b, s, DIFF_HEADS, 2, DIFF_HEAD_DIM), DIFF_ROT)
        k_b = _rotary(k_d.reshape(b, s, DIFF_HEADS, 2, DIFF_HEAD_DIM), DIFF_ROT)
        v_b = v_d.reshape(b, s, DIFF_HEADS, 2 * DIFF_HEAD_DIM)
        lam_init = 0.8 - 0.6 * math.exp(-0.3 * l)
        lp = diff_lambda[l].astype(jnp.float32)
        lam = jnp.exp(jnp.sum(lp[0] * lp[1])) - jnp.exp(jnp.sum(lp[2] * lp[3])) + lam_init
        o_b = _differential_attention(q_b.transpose(0, 2, 3, 1, 4), k_b.transpose(0, 2, 3, 1, 4),
                                      v_b.transpose(0, 2, 1, 3), lam, diff_scale)
        o_b = _rms_norm(o_b, g_diff[l], eps=1e-5) * (1.0 - lam_init)
        y_b = o_b.transpose(0, 2, 1, 3).reshape(b, s, DIFF_WIDTH) * jax.nn.silu(gate_b)
        y_b = y_b @ w_branch_b[l]

        g = jax.nn.sigmoid(gates + b_merge[l])
        merged = g[..., :D_MODEL] * y_a + g[..., D_MODEL:] * y_b
        out = merged @ w_out[l]
        x = _layer_norm(DEEPNORM_ALPHA * x + out, ln_gamma[l], ln_beta[l])
    return x
```

```python
import math
from contextlib import ExitStack

import numpy as np
import concourse.bass as bass
import concourse.mybir as mybir
from concourse.bass_utils import run_bass_kernel_spmd

F32 = mybir.dt.float32
BF16 = mybir.dt.bfloat16
AF = mybir.ActivationFunctionType
ALU = mybir.AluOpType

D = 1024
S = 4096
SO = 2048
NB = 8
NBO = 4
DEPTH = 2
IN_COLS = 5024
ALPHA = (2 * DEPTH) ** 0.25
LN_EPS = 1e-5
RMS_EPS = 1e-6
ROPE_THETA = 500000.0
MLA_SCALE = 1.0 / math.sqrt(96.0)
DIFF_SCALE = 1.0 / math.sqrt(64.0)
C_CQ, C_CKV, C_KR, C_GA, C_QD, C_KD, C_VD, C_GB, C_G = 0, 256, 384, 416, 928, 1440, 1952, 2464, 2976

ENGS = ("pe", "act", "dve", "pool", "sp")
DEBUG = False


class Buf:
    __slots__ = ("name", "w", "r")

    def __init__(self, name=""):
        self.name = name
        self.w = None
        self.r = []


class Chan:
    __slots__ = ("sem", "val")

    def __init__(self, sem):
        self.sem = sem
        self.val = 0


class Prog:
    def __init__(self, nc, es, same_engine_sync=True):
        self.nc = nc
        self.es = es
        self.same = same_engine_sync
        self.ops = {e: [] for e in ENGS}
        self.esem = {e: es.enter_context(nc.semaphore(f"s_{e}")) for e in ENGS if e != "sp"}
        self.nchan = 0
        self.rings = {}
        self.ring_i = {}

    def ring_chan(self, q, k=None):
        if q not in self.rings:
            n = {"pool": 12, "sp": 8}.get(q, 4)
            self.rings[q] = [self.chan() for _ in range(n)]
            self.ring_i[q] = 0
        r = self.rings[q]
        c = r[self.ring_i[q] % len(r)]
        self.ring_i[q] += 1
        return c

    def chan(self):
        self.nchan += 1
        return Chan(self.es.enter_context(self.nc.semaphore(f"ch{self.nchan}")))

    def sbuf(self, name, shape, dtype):
        return self.es.enter_context(self.nc.sbuf_tensor("sb_" + name, list(shape), dtype))

    def psum(self, name, shape, dtype):
        return self.es.enter_context(self.nc.psum_tensor("ps_" + name, list(shape), dtype))

    def _deps(self, eng, reads, writes):
        toks = []
        for b in reads:
            if b.w is not None:
                toks.append(b.w)
        for b in writes:
            if b.w is not None:
                toks.append(b.w)
            toks.extend(b.r)
        best = {}
        for t in toks:
            if t[0] == "eng" and t[1] == eng and (eng in ("pe", "sp") or not self.same):
                continue
            k = (t[0], t[1] if t[0] == "eng" else id(t[1]))
            if k not in best or best[k][2] < t[2]:
                best[k] = t
        return list(best.values())

    def _finish(self, tok, reads, writes):
        key = tok[1]
        for b in reads:
            b.r = [t for t in b.r if not (t[0] == tok[0] and (t[1] == key if tok[0] == "eng" else t[1] is key))]
            b.r.append(tok)
        for b in writes:
            b.w = tok
            b.r = []
        return tok

    def op(self, eng, fn, reads=(), writes=()):
        deps = self._deps(eng, reads, writes)
        idx = len(self.ops[eng])
        self.ops[eng].append({"fn": fn, "deps": deps, "inc": False, "dma": None})
        return self._finish(("eng", eng, idx), reads, writes)

    def dma(self, eng, chan, fn, reads=(), writes=(), inc=16):
        deps = self._deps(eng, reads, writes)
        if chan.val > 0:
            deps.append(("dma", chan, chan.val))
        chan.val += inc
        self.ops[eng].append({"fn": fn, "deps": deps, "inc": False, "dma": chan, "dinc": inc})
        return self._finish(("dma", chan, chan.val), reads, writes)

    def retire(self, alias_bufs, canon_bufs):
        for a in alias_bufs:
            toks = list(a.r)
            if a.w is not None:
                toks.append(a.w)
            for b in canon_bufs:
                b.r.extend(toks)
            a.w = None
            a.r = []

    def wait_tok(self, eng, toks):
        self.ops[eng].append({"fn": None, "deps": list(toks), "inc": False, "dma": None})

    def emit(self, block):
        for e in ENGS:
            for rec in self.ops[e]:
                for t in rec["deps"]:
                    if t[0] == "eng":
                        self.ops[t[1]][t[2]]["inc"] = True
        cnt = {}
        for e in ENGS:
            c = 0
            lst = []
            for rec in self.ops[e]:
                if rec["inc"]:
                    c += 1
                lst.append(c)
            cnt[e] = lst

        def run(e, engine):
            seen = {}
            for rec in self.ops[e]:
                need = {}
                for t in rec["deps"]:
                    if t[0] == "eng":
                        sem = self.esem[t[1]]
                        val = cnt[t[1]][t[2]]
                    else:
                        sem = t[1].sem
                        val = t[2]
                    k = id(sem)
                    if seen.get(k, 0) >= val:
                        continue
                    if k not in need or need[k][1] < val:
                        need[k] = (sem, val)
                for k, (sem, val) in need.items():
                    engine.wait_ge(sem, val)
                    seen[k] = val
                if rec["fn"] is None:
                    continue
                ins = rec["fn"](engine)
                if rec["dma"] is not None:
                    ins.then_inc(rec["dma"].sem, rec.get("dinc", 16))
                elif rec["inc"]:
                    ins.then_inc(self.esem[e], 1)

        @block.tensor
        def _(eng):
            run("pe", eng)

        @block.scalar
        def _(eng):
            run("act", eng)

        @block.vector
        def _(eng):
            run("dve", eng)

        @block.gpsimd
        def _(eng):
            run("pool", eng)

        @block.sync
        def _(eng):
            run("sp", eng)


def build(n_layers=1, final_out=True):
    nc = bass.Bass("TRN2", target_bir_lowering=False)
    dt_in = lambda name, shape: nc.dram_tensor(name, list(shape), F32, kind="ExternalInput").ap()
    xT_d = dt_in("xT", [D, S])
    xo_d = dt_in("xo", [SO, D])
    cs_d = dt_in("cs", [2, 128, S])
    pm_d = dt_in("pm", [128, 128])
    id_d = dt_in("ident", [128, 128])
    sel_d = nc.dram_tensor("sel", [1, 2], mybir.dt.int32, kind="ExternalInput").ap()
    x1o_t = nc.dram_tensor("x1o", [SO, D], F32)
    x1s_t = [nc.dram_tensor(f"x1s{i}", [D, 512], BF16) for i in range(NBO)]
    G_t = [nc.dram_tensor(f"G{i}", [2 * D, 512], BF16) for i in range(NBO)]
    L = []
    for l in range(n_layers):
        L.append(dict(
            w_in=dt_in(f"w_in{l}", [D, 416]),
            wt=dt_in(f"wt{l}", [36, 128, 1024]),
            wqp=dt_in(f"wqp{l}", [256, 768]),
            wkp=dt_in(f"wkp{l}", [128, 768]),
            wvp=dt_in(f"wvp{l}", [128, 512]),
            w_ba=dt_in(f"w_ba{l}", [512, D]),
            w_bb=dt_in(f"w_bb{l}", [512, D]),
            w_out=dt_in(f"w_out{l}", [D, D]),
            vecs=dt_in(f"vecs{l}", [128, 20]),
            lam=dt_in(f"lam{l}", [128, 256]),
            lnp=dt_in(f"lnp{l}", [128, 2 * D]),
            lam_init=0.8 - 0.6 * math.exp(-0.3 * l),
        ))
    y_d = nc.dram_tensor("y", [SO, D], F32, kind="ExternalOutput").ap()
    dbg_d = nc.dram_tensor("dbg", [D, S], BF16, kind="ExternalOutput").ap() if DEBUG else None
    return nc, dict(xT=xT_d, xo=xo_d, cs=cs_d, pm=pm_d, ident=id_d, sel=sel_d, x1o=x1o_t, x1s=x1s_t, G=G_t, L=L, y=y_d, dbg=dbg_d)


def emit_program(nc, T, lam_inits):
    with ExitStack() as es:
        P = Prog(nc, es)
        _emit(nc, P, T, lam_inits)
        with nc.Block() as block:
            P.emit(block)


def _emit(nc, P, T, lam_inits):
    def MM(out, lhsT, rhs, st, sp, R, W):
        return P.op("pe", lambda e: e.matmul(out, lhsT, rhs, start=st, stop=sp), R, W)

    def ACT(out, in_, func, R, W, bias=None, scale=None):
        kw = {}
        if bias is not None:
            kw["bias"] = bias
        if scale is not None:
            kw["scale"] = scale
        return P.op("act", lambda e: e.activation(out, in_, func, **kw), R, W)

    def TT(eng, out, in0, in1, op, R, W):
        return P.op(eng, lambda e: e.tensor_tensor(out, in0, in1, op), R, W)

    def TS(eng, out, in0, s1, s2, op0, op1, R, W):
        if op1 is None:
            return P.op(eng, lambda e: e.tensor_scalar(out, in0, s1, None, op0), R, W)
        return P.op(eng, lambda e: e.tensor_scalar(out, in0, s1, s2, op0, op1), R, W)

    def STT(eng, out, in0, sc, in1, op0, op1, R, W):
        return P.op(eng, lambda e: e.scalar_tensor_tensor(out, in0, sc, in1, op0, op1), R, W)

    def CP(eng, out, in_, R, W):
        return P.op(eng, lambda e: e.tensor_copy(out, in_), R, W)

    def RECIP(out, in_, R, W):
        return P.op("dve", lambda e: e.reciprocal(out, in_), R, W)

    def MEMSET(eng, ap, val, W):
        return P.op(eng, lambda e: e.memset(ap, val), (), W)

    def DMA(q, out, in_, R, W, ch=None):
        ch = ch or P.ring_chan(q)
        return P.dma(q, ch, lambda e: e.dma_start(out=out, in_=in_), R, W)

    xT = P.sbuf("xT", [128, 8, S], BF16)
    b_xT = [Buf(f"xT{i}") for i in range(NB)]
    Ct = P.sbuf("Ct", [128, S], BF16)
    St = P.sbuf("St", [128, S], BF16)
    b_cs = Buf("cs")
    pm = P.sbuf("pm", [128, 128], BF16)
    ones = P.sbuf("ones", [128, 128], BF16)
    b_const = Buf("const")
    eps_r = P.sbuf("eps_r", [128, 1], F32)
    eps_l = P.sbuf("eps_l", [128, 1], F32)
    arena1 = P.sbuf("arena1", [128, 8192], BF16)
    cn = arena1[:, 0:4096]
    cqn = arena1[:, 4096:8192]
    b_cn = [Buf(f"cn{i}") for i in range(NB)]
    b_cqn = [Buf(f"cqn{i}") for i in range(NBO)]
    KAB = P.sbuf("KAB", [128, 2, S], BF16)
    b_K = [[Buf(f"K{j}_{i}") for i in range(NB)] for j in range(2)]
    QAB = P.sbuf("QAB", [128, 2, SO], BF16)
    b_Q = [[Buf(f"Q{j}_{i}") for i in range(NBO)] for j in range(2)]
    VAB = P.sbuf("VAB", [128, 2, 32, 128], BF16)
    b_V = [[Buf(f"V{j}_{i}") for i in range(4)] for j in range(2)]
    ya = P.sbuf("ya", [128, 4, SO], BF16)
    yb = P.sbuf("yb", [128, 4, SO], BF16)
    b_ya = [[Buf() for _ in range(NBO)] for _ in range(4)]
    b_yb = [[Buf() for _ in range(NBO)] for _ in range(4)]
    NPT = 3
    Pt = [P.sbuf(f"Pt{i}", [128, 2, 512], BF16) for i in range(NPT)]
    b_Pt = [Buf(f"Pt{i}") for i in range(NPT)]
    w_lat = P.sbuf("w_lat", [128, 8, 416], BF16); b_wlat = Buf()
    wqp = P.sbuf("wqp", [128, 2, 768], BF16); b_wqp = Buf()
    wkp = P.sbuf("wkp", [128, 768], BF16); b_wkp = Buf()
    wvp = P.sbuf("wvp", [128, 512], BF16); b_wvp = Buf()
    w_lat_flat = w_lat[:, :, :].rearrange("p c n -> p (c n)")
    wga = [w_lat_flat[:, i * 1024:(i + 1) * 1024].rearrange("p (c n) -> p c n", c=8) for i in range(2)]
    b_wga = [Buf(), Buf()]
    sga = P.sbuf("sga", [128, 4, 512], F32)
    b_sga = [Buf() for _ in range(4)]
    vecs = P.sbuf("vecs", [128, 20], F32); b_vecs = Buf()
    sm = P.sbuf("sm", [128, 32], F32); b_sm = Buf()
    wk_f = [P.sbuf(f"wkf{i}", [128, 512], F32) for i in range(4)]
    b_wkf = [Buf() for _ in range(4)]
    wk_g = [w_lat_flat[:, 2048:3072].bitcast(F32), P.sbuf("wkg1", [128, 512], F32)]
    b_wkg = [Buf() for _ in range(2)]
    wk_b = [P.sbuf(f"wkb{i}", [128, 512], BF16) for i in range(2)]
    lam = wk_f[1][:, 0:256]; b_lam = b_wkf[1]
    b_wkb = [Buf() for _ in range(2)]

    stats = P.sbuf("stats", [128, 2, 6], F32); b_stats = Buf()
    mv = P.sbuf("mv", [128, 4], F32); b_mv = Buf()

    ps = P.psum("ps", [128, 8, 512], F32)
    b_ps = [Buf(f"ps{i}") for i in range(8)]
    SB = [(ps[:, 0:2, :], [b_ps[0], b_ps[1]]), (ps[:, 2:4, :], [b_ps[2], b_ps[3]])]
    ACC = [(ps[:, 4, :], b_ps[4]), (ps[:, 5, :], b_ps[5])]
    XY = [(ps[:, 6, :], b_ps[6]), (ps[:, 7, :], b_ps[7])]
    xy_i = [0]
    ALLB = [(ps[:, i, :], b_ps[i]) for i in (6, 7, 0, 1, 2, 3, 5)]
    ring = {"banks": XY}

    def nxt_bg():
        return XY[1]

    def nxt_xy():
        r = ring["banks"][xy_i[0] % len(ring["banks"])]
        xy_i[0] += 1
        return r

    MEMSET("dve", ones[:], 1.0, [b_const])
    MEMSET("dve", eps_r[:], RMS_EPS, [b_const])
    MEMSET("dve", eps_l[:], LN_EPS, [b_const])
    DMA("pool", pm[:], T["pm"], [], [b_const])
    ident = P.sbuf("ident", [128, 128], BF16)
    DMA("pool", ident[:], T["ident"], [], [b_const])
    sel_sb = P.sbuf("sel_sb", [1, 2], mybir.dt.int32); b_sel = Buf()
    DMA("sp", sel_sb[:], T["sel"], [], [b_sel])
    b_x1o = [Buf() for _ in range(16)]
    b_x1s = [Buf() for _ in range(NBO)]
    b_G = [Buf() for _ in range(NBO)]
    xregs = [nc.gpsimd.alloc_register(f"roff{i}") for i in range(1)]
    n_layers = len(T["L"])
    xT_src = T["xT"].rearrange("(c p) s -> p c s", p=128)
    DMA("pool", xT[:, :, 0:512], xT_src[:, :, 0:512], [], [b_xT[0]])

    def late_const_loads():
        DMA("pool", Ct[:], T["cs"][0], [], [b_cs])
        DMA("pool", St[:], T["cs"][1], [], [b_cs])
        for blk in range(1, NB):
            DMA("pool", xT[:, :, blk * 512:(blk + 1) * 512], xT_src[:, :, blk * 512:(blk + 1) * 512], [], [b_xT[blk]])

    for l, Lw in enumerate(T["L"]):
        lam_init = lam_inits[l]
        if DEBUG and l == 1:
            out_dbg = DMA("sp", T["dbg"].rearrange("(c p) s -> p c s", p=128), xT[:, :, :], list(b_xT), [])
        w_in_v = Lw["w_in"].rearrange("(c p) n -> p c n", p=128)
        wt_d = Lw["wt"]
        P.retire(b_wga + [b_wkg[0]], [b_wlat])
        DMA("pool", w_lat[:], w_in_v[:, :, 0:416], [], [b_wlat])
        DMA("sp", vecs[:], Lw["vecs"], [], [b_vecs])
        DMA("sp", lam[:], Lw["lam"], [], [b_lam])
        DMA("pool", wqp[:], Lw["wqp"].rearrange("(c p) n -> p c n", p=128), [], [b_wqp])
        DMA("pool", wkp[:], Lw["wkp"], [], [b_wkp])
        DMA("pool", wvp[:], Lw["wvp"], [], [b_wvp])
        if l == 0:
            late_const_loads()

        lprod = wk_f[0]
        TT("dve", lprod[:, 0:128], lam[:, 0:128], lam[:, 128:256], ALU.mult, [b_lam], [b_wkf[0]])
        P.op("dve", lambda e: e.tensor_reduce(sm[:, 20:22], lprod[:, 0:128].rearrange("p (a b) -> p a b", a=2),
                                              mybir.AxisListType.X, ALU.add), [b_wkf[0]], [b_sm])
        ACT(sm[:, 22:24], sm[:, 20:22], AF.Exp, [b_sm], [b_sm])
        TT("dve", sm[:, 24:25], sm[:, 23:24], sm[:, 22:23], ALU.subtract, [b_sm], [b_sm])
        TS("dve", sm[:, 0:1], sm[:, 24:25], -lam_init, None, ALU.add, None, [b_sm], [b_sm])
        TS("dve", sm[:, 1:2], vecs[:, 3:4], (1.0 - lam_init) * 0.5, None, ALU.mult, None, [b_vecs, b_sm], [b_sm])
        TS("dve", sm[:, 2:18], vecs[:, 4:20], 0.5, None, ALU.mult, None, [b_vecs, b_sm], [b_sm])

        ring["banks"] = ALLB
        def lat_block(blk):
            cols = slice(blk * 512, (blk + 1) * 512)
            pX, bX = nxt_xy()
            for c in range(8):
                MM(pX, w_lat[:, c, 256:384], xT[:, c, cols], c == 0, c == 7, [b_wlat, b_xT[blk]], [bX])
            sq, bsq = wk_b[0], b_wkb[0]
            ACT(sq[:], pX, AF.Square, [bX], [bsq])
            pY, bY = nxt_xy()
            MM(pY, ones[:], sq[:], True, True, [b_const, bsq], [bY])
            rs, brs = wk_f[1], b_wkf[1]
            ACT(rs[:], pY, AF.Sqrt, [bY, b_const], [brs], bias=eps_r[:], scale=1.0 / 128.0)
            RECIP(rs[:], rs[:], [brs], [brs])
            STT("dve", cn[:, cols], pX, vecs[:, 0:1], rs[:], ALU.mult, ALU.mult, [bX, b_vecs, brs], [b_cn[blk]])
            pX, bX = nxt_xy()
            for c in range(8):
                MM(pX[0:64, :], w_lat[:, c, 352:416], xT[:, c, cols], c == 0, c == 7, [b_wlat, b_xT[blk]], [bX])
            ab, bab = wk_b[1], b_wkb[1]
            CP("dve", ab[0:64, :], pX[0:64, :], [bX], [bab])
            pY, bY = nxt_xy()
            MM(pY[0:64, :], pm[0:64, 0:64], ab[0:64, :], True, True, [b_const, bab], [bY])
            t1, bt1 = wk_f[2], b_wkf[2]
            t2, bt2 = wk_f[3], b_wkf[3]
            TT("dve", t1[32:64, :], pX[32:64, :], Ct[32:64, cols], ALU.mult, [bX, b_cs], [bt1])
            TT("dve", t2[32:64, :], pY[32:64, :], St[32:64, cols], ALU.mult, [bY, b_cs], [bt2])
            TT("dve", KAB[32:64, 0, cols], t1[32:64, :], t2[32:64, :], ALU.add, [bt1, bt2], [b_K[0][blk]])
            TT("dve", KAB[32:64, 1, cols], t1[32:64, :], t2[32:64, :], ALU.add, [bt1, bt2], [b_K[1][blk]])

        def cq_block(blk):
            cols = slice(blk * 512, (blk + 1) * 512)
            pc = []
            for j in range(2):
                pX, bX = nxt_xy()
                for c in range(8):
                    MM(pX, w_lat[:, c, j * 128:(j + 1) * 128], xT[:, c, cols], c == 0, c == 7, [b_wlat, b_xT[blk]], [bX])
                pc.append((pX, bX))
            pZ, bZ = ACC[0]
            for j in range(2):
                sq, bsq = wk_b[j], b_wkb[j]
                ACT(sq[:], pc[j][0], AF.Square, [pc[j][1]], [bsq])
                MM(pZ, ones[:], sq[:], j == 0, j == 1, [b_const, bsq], [bZ])
            rs, brs = wk_f[1], b_wkf[1]
            ACT(rs[:], pZ, AF.Sqrt, [bZ, b_const], [brs], bias=eps_r[:], scale=1.0 / 256.0)
            RECIP(rs[:], rs[:], [brs], [brs])
            for j in range(2):
                STT("dve", cqn[:, j * 2048 + blk * 512: j * 2048 + (blk + 1) * 512], pc[j][0], vecs[:, 1 + j:2 + j], rs[:],
                    ALU.mult, ALU.mult, [pc[j][1], b_vecs, brs], [b_cqn[blk]])


        for blk in range(NBO):
            lat_block(blk)
        for blk in range(NBO):
            cq_block(blk)
        for blk in range(NBO, NB):
            lat_block(blk)
        ring["banks"] = XY
        for g in range(4):
            MEMSET("pool", VAB[:, 0, g * 8:(g + 1) * 8, 64:128], 1.0, [b_V[0][g]])
            MEMSET("pool", VAB[:, 1, g * 8:(g + 1) * 8, 0:64], 1.0, [b_V[1][g]])

        dWv = [arena1[:, 0:4096].rearrange("p (w c n) -> p w c n", w=4, c=8),
               arena1[:, 4096:8192].rearrange("p (w c n) -> p w c n", w=4, c=8)]
        b_dW = [b_cn, b_cqn]

        def mla_prod(h):
            j = h % 2
            Kv = KAB[:, j, :]
            Qv = QAB[:, j, :]
            Vv = VAB[:, j, :, :]
            for blk in range(NB):
                cols = slice(blk * 512, (blk + 1) * 512)
                pX, bX = nxt_xy()
                MM(pX[0:96, :], wkp[:, h * 96:(h + 1) * 96], cn[:, cols], True, True, [b_wkp, b_cn[blk]], [bX])
                CP("dve", Kv[0:32, cols], pX[0:32, :], [bX], [b_K[j][blk]])
                CP("dve", Kv[64:96, cols], pX[64:96, :], [bX], [b_K[j][blk]])
                yield
            voff = 0 if j == 0 else 64
            for g in range(4):
                pX, bX = nxt_xy()
                for i in range(8):
                    kt = g * 8 + i
                    MM(pX[:, i * 64:(i + 1) * 64], cn[:, kt * 128:(kt + 1) * 128], wvp[:, h * 64:(h + 1) * 64], True, True,
                       [b_cn[kt // 4], b_wvp], [bX])
                CP("dve", Vv[:, g * 8:(g + 1) * 8, voff:voff + 64], pX.rearrange("p (a b) -> p a b", a=8), [bX], [b_V[j][g]])
                yield
            for qb in range(NBO):
                cols = slice(qb * 512, (qb + 1) * 512)
                pX, bX = nxt_xy()
                for c in range(2):
                    MM(pX[0:96, :], wqp[:, c, h * 96:(h + 1) * 96], cqn[:, c * 2048 + qb * 512: c * 2048 + (qb + 1) * 512],
                       c == 0, c == 1, [b_wqp, b_cqn[qb]], [bX])
                CP("dve", Qv[0:96, cols], pX[0:96, :], [bX], [b_Q[j][qb]])
                t1, bt1 = wk_g[0], b_wkg[0]
                t2, bt2 = wk_g[1], b_wkg[1]
                TT("dve", t1[32:64, :], pX[32:64, :], Ct[32:64, cols], ALU.mult, [bX, b_cs], [bt1, b_wlat])
                yield
                pY, bY = nxt_xy()
                MM(pY[0:96, :], pm[0:96, 0:96], Qv[0:96, cols], True, True, [b_const, b_Q[j][qb]], [bY])
                TT("dve", t2[32:64, :], pY[32:64, :], St[32:64, cols], ALU.mult, [bY, b_cs], [bt2])
                TT("dve", Qv[32:64, cols], t1[32:64, :], t2[32:64, :], ALU.add, [bt1, bt2], [b_Q[j][qb]])
                yield

        def diff_prod(h):
            j = h % 2
            Kv = KAB[:, j, :]
            Qv = QAB[:, j, :]
            Vv = VAB[:, j, :, :]
            dW = dWv[j]
            bdW = b_dW[j]
            for w in range(4):
                DMA("pool", arena1[:, j * 4096 + w * 1024: j * 4096 + (w + 1) * 1024], wt_d[4 + w * 4 + h], [], bdW)
            yield

            def qk_prod(w, dst, bdst, nblk):
                for blk in range(nblk):
                    cols = slice(blk * 512, (blk + 1) * 512)
                    pX, bX = nxt_xy()
                    for c in range(8):
                        MM(pX, dW[:, w, c, :], xT[:, c, cols], c == 0, c == 7, bdW + [b_xT[blk]], [bX])
                    CP("dve", dst[:, cols], pX, [bX], [bdst[blk]])
                    t1, bt1 = wk_g[0], b_wkg[0]
                    t2, bt2 = wk_g[1], b_wkg[1]
                    for r0 in (0, 64):
                        TT("dve", t1[r0:r0 + 16, :], pX[r0:r0 + 16, :], Ct[r0:r0 + 16, cols], ALU.mult, [bX, b_cs], [bt1, b_wlat])
                    yield
                    pY, bY = nxt_xy()
                    MM(pY, pm[:], dst[:, cols], True, True, [b_const, bdst[blk]], [bY])
                    for r0 in (0, 64):
                        TT("dve", t2[r0:r0 + 16, :], pY[r0:r0 + 16, :], St[r0:r0 + 16, cols], ALU.mult, [bY, b_cs], [bt2])
                        TT("dve", dst[r0:r0 + 16, cols], t1[r0:r0 + 16, :], t2[r0:r0 + 16, :], ALU.add, [bt1, bt2], [bdst[blk]])
                    yield

            yield from qk_prod(1, Kv, b_K[j], NB)
            yield from qk_prod(0, Qv, b_Q[j], NBO)
            for kt in range(32):
                pX, bX = nxt_xy()
                for c in range(8):
                    MM(pX[:, 0:128], xT[:, c, kt * 128:(kt + 1) * 128], dW[:, 2, c, :], c == 0, c == 7,
                       bdW + [b_xT[kt // 4]], [bX])
                CP("dve", Vv[:, kt, :], pX[:, 0:128], [bX], [b_V[j][kt // 8]])
                yield

        def drain(gen):
            if gen is not None:
                for _ in gen:
                    pass

        def attention_pass(Kv, bK, kdim_lo, kdim_hi, Qv, bQ, pv_list, scale, bg=None, sum_acc=None, pre=None):
            NP = 16

            def emit_S(kp):
                sb, bsb = SB[kp % 2]
                for i in range(2):
                    kt = 2 * kp + i
                    MM(sb[:, i, :], Kv[kdim_lo:kdim_hi, kt * 128:(kt + 1) * 128], Qv[kdim_lo:kdim_hi, :], True, True,
                       [bK[kt // 4], bQ], [bsb[i]])

            emit_S(0)
            emit_S(1)
            for kp in range(NP):
                sb, bsb = SB[kp % 2]
                pt, bpt = Pt[kp % NPT], b_Pt[kp % NPT]
                ACT(pt[:], sb, AF.Exp, bsb, [bpt], scale=scale)
                if kp + 2 < NP:
                    emit_S(kp + 2)
                for i in range(2):
                    kt = 2 * kp + i
                    for (lfn, acc, bacc) in pv_list:
                        lap, lb = lfn(kt)
                        MM(acc, lap, pt[:, i, :], kt == 0, kt == 31, lb + [bpt], [bacc])
                if sum_acc is not None:
                    pp, bpp = wk_b[kp % 2], b_wkb[kp % 2]
                    TT("dve", pp[:], pt[:, 0, :], pt[:, 1, :], ALU.add, [bpt], [bpp])
                    if kp >= 1:
                        MM(sum_acc[0], ones[:], wk_b[(kp - 1) % 2][:], kp == 1, False, [b_const, b_wkb[(kp - 1) % 2]], [sum_acc[1]])
                    if kp == NP - 1:
                        MM(sum_acc[0], ones[:], pp[:], False, True, [b_const, bpp], [sum_acc[1]])
                if pre is not None and kp == 2:
                    pre()
                if bg is not None and (pre is None or kp >= 2):
                    next(bg, None)

        drain(mla_prod(0))
        for h in range(8):
            j = h % 2
            pair = h // 2
            Kv = KAB[:, j, :]
            Qv = QAB[:, j, :]
            Vv = VAB[:, j, :, :]
            if j == 0:
                DMA("pool", w_lat_flat[:, (pair % 2) * 1024:(pair % 2 + 1) * 1024], wt_d[pair], [],
                    [b_wga[pair % 2]] + ([b_wlat] if pair < 2 else []))
            bg = mla_prod(h + 1) if h < 7 else diff_prod(0)
            for qb in range(NBO):
                cols = slice(qb * 512, (qb + 1) * 512)
                if j == 0:
                    pX, bX = nxt_xy()
                    for c in range(8):
                        MM(pX, wga[pair % 2][:, c, :], xT[:, c, cols], c == 0, c == 7, [b_wga[pair % 2], b_xT[qb]], [bX])
                    tg, btg = wk_f[0], b_wkf[0]
                    ACT(tg[:], pX, AF.Tanh, [bX], [btg], scale=0.5)
                    STT("dve", sga[:, qb, :], tg[:], 1.0, pX, ALU.add, ALU.mult, [btg, bX], [b_sga[qb]])
                acc, bacc = ACC[(h * 4 + qb) % 2]
                attention_pass(Kv, b_K[j], 0, 96, Qv[:, cols], b_Q[j][qb],
                               [(lambda kt, Vv=Vv, j=j: (Vv[:, kt, :], [b_V[j][kt // 8]]), acc, bacc)], MLA_SCALE, bg=bg)
                o_lo, s_lo = (0, 64) if j == 0 else (64, 0)
                rc, brc = wk_f[1], b_wkf[1]
                RECIP(rc[s_lo:s_lo + 64, :], acc[s_lo:s_lo + 64, :], [bacc], [brc])
                tt, btt = wk_f[2], b_wkf[2]
                TT("dve", tt[o_lo:o_lo + 64, :], acc[o_lo:o_lo + 64, :], rc[s_lo:s_lo + 64, :], ALU.mult, [bacc, brc], [btt])
                STT("dve", ya[o_lo:o_lo + 64, pair, cols], tt[o_lo:o_lo + 64, :], 0.5, sga[o_lo:o_lo + 64, qb, :], ALU.mult, ALU.mult,
                    [btt, b_sga[qb]], [b_ya[pair][qb]])
            drain(bg)

        pending = [None]

        def make_fin(h, qb, dW, bdW):
            cols = slice(qb * 512, (qb + 1) * 512)

            def fin():
                o, bo = wk_f[2], b_wkf[2]
                sq = wk_f[0][:, :].bitcast(BF16)[:, 0:512]
                bsq = b_wkf[0]
                pX, bX = nxt_xy()
                MM(pX, ones[:], sq, True, True, [b_const, bsq], [bX])
                rs, brs = wk_f[1], b_wkf[1]
                ACT(rs[:], pX, AF.Sqrt, [bX, b_const], [brs], bias=eps_l[:], scale=1.0 / 128.0)
                RECIP(rs[:], rs[:], [brs], [brs])
                pY, bY = nxt_xy()
                for c in range(8):
                    MM(pY, dW[:, 3, c, :], xT[:, c, cols], c == 0, c == 7, bdW + [b_xT[qb]], [bY])
                tg, btg = wk_f[0], b_wkf[0]
                ACT(tg[:], pY, AF.Tanh, [bY], [btg], scale=0.5)
                sg, bsg = wk_f[3], b_wkf[3]
                STT("dve", sg[:], tg[:], 1.0, pY, ALU.add, ALU.mult, [btg, bY], [bsg])
                STT("dve", o[:], o[:], sm[:, 1:2], rs[:], ALU.mult, ALU.mult, [bo, b_sm, brs], [bo])
                TT("dve", yb[:, h, cols], o[:], sg[:], ALU.mult, [bo, bsg], [b_yb[h][qb]])
            return fin

        def take_pending():
            f = pending[0]
            pending[0] = None
            return f

        for h in range(4):
            j = h % 2
            Kv = KAB[:, j, :]
            Qv = QAB[:, j, :]
            Vv = VAB[:, j, :, :]
            dW = dWv[j]
            bdW = b_dW[j]
            bg = diff_prod(h + 1) if h < 3 else None
            for qb in range(NBO):
                cols = slice(qb * 512, (qb + 1) * 512)
                tm = []
                for m in range(2):
                    acc, bacc = ACC[0]
                    sacc, bsacc = ACC[1]
                    attention_pass(Kv, b_K[j], m * 64, (m + 1) * 64, Qv[:, cols], b_Q[j][qb],
                                   [(lambda kt, Vv=Vv, j=j: (Vv[:, kt, :], [b_V[j][kt // 8]]), acc, bacc)],
                                   DIFF_SCALE, bg=bg, sum_acc=(sacc, bsacc), pre=take_pending() if m == 0 else None)
                    rc, brc = wk_f[1], b_wkf[1]
                    t, bt = wk_f[2 + m], b_wkf[2 + m]
                    ACT(t[:], acc, AF.Copy, [bacc], [bt])
                    CP("dve", rc[:], sacc, [bsacc], [brc])
                    RECIP(rc[:], rc[:], [brc], [brc])
                    TT("dve", t[:], t[:], rc[:], ALU.mult, [bt, brc], [bt])
                    tm.append((t, bt))
                o, bo = wk_f[2], b_wkf[2]
                STT("dve", o[:], tm[1][0][:], sm[:, 0:1], tm[0][0][:], ALU.mult, ALU.add, [tm[1][1], tm[0][1], b_sm], [bo])
                sq = wk_f[0][:, :].bitcast(BF16)[:, 0:512]
                ACT(sq, o[:], AF.Square, [bo], [b_wkf[0]])
                pending[0] = make_fin(h, qb, dW, bdW)
            drain(bg)
        f = take_pending()
        if f is not None:
            f()

        ring["banks"] = ALLB + [(ps[:, 4, :], b_ps[4])]
        all_K = [b for lst in b_K for b in lst]
        all_Q = [b for lst in b_Q for b in lst]
        all_V = [b for lst in b_V for b in lst]
        w_out_sb = KAB[:, :, :].rearrange("p a (c n) -> p (a c) n", n=1024)
        w_ba_sb = VAB[:, 0, :, :].rearrange("p (c a) n -> p c (a n)", c=4)
        w_bb_sb = VAB[:, 1, :, :].rearrange("p (c a) n -> p c (a n)", c=4)
        DMA("pool", w_out_sb, Lw["w_out"].rearrange("(c p) n -> p c n", p=128), [], all_K)
        TS("dve", w_out_sb, w_out_sb, 0.5, None, ALU.mult, None, all_K, all_K)
        DMA("pool", w_ba_sb, Lw["w_ba"].rearrange("(c p) n -> p c n", p=128), [], b_V[0])
        DMA("pool", w_bb_sb, Lw["w_bb"].rearrange("(c p) n -> p c n", p=128), [], b_V[1])
        lnp = QAB[:, :, :].rearrange("p a n -> p (a n)").bitcast(F32)
        DMA("sp", lnp, Lw["lnp"], [], all_Q)
        mergedv = [arena1[:, 0:4096].rearrange("p (c n) -> p c n", c=8), arena1[:, 4096:8192].rearrange("p (c n) -> p c n", c=8)]
        b_mergedv = [[Buf("merged0")], [Buf("merged1")]]
        first_use = {0: True, 1: True}
        NWG = NPT
        wg = [Pt[i][:, :, :].rearrange("p a n -> p (a n)").rearrange("p (c n) -> p c n", c=8) for i in range(NWG)]
        b_wg = [[b_Pt[i]] for i in range(NWG)]
        xt_sb = sga[:, 0:2, :].rearrange("p a n -> p (a n)")
        zt_sb = sga[:, 2:4, :].rearrange("p a n -> p (a n)")
        b_xt = b_sga[0:2]
        b_zt = b_sga[2:4]
        if l == 0:
            out_toks = []
        items = [(blk, oc, br) for blk in range(NBO) for oc in range(8) for br in range(2)]

        def load_wg(i):
            blk_, oc_, br_ = items[i]
            DMA("pool", Pt[i % NWG][:, :, :].rearrange("p a n -> p (a n)"), wt_d[20 + br_ * 8 + oc_], [], b_wg[i % NWG])

        def emit_tile(blk, t):
            merged, b_merged = mergedv[blk % 2], b_mergedv[blk % 2]
            tok0 = blk * 512 + t * 128
            ti = blk * 4 + t
            if l == 0:
                DMA("sp", xt_sb, T["xo"][tok0:tok0 + 128, :], [], b_xt)
            else:
                DMA("sp", xt_sb, T["x1o"].ap()[tok0:tok0 + 128, :], [b_x1o[ti]], b_xt)
            for half in range(2):
                pX, bX = nxt_xy()
                for c in range(8):
                    MM(pX, merged[:, c, t * 128:(t + 1) * 128], w_out_sb[:, c, half * 512:(half + 1) * 512], c == 0, c == 7,
                       b_merged + all_K, [bX])
                STT("dve", zt_sb[:, half * 512:(half + 1) * 512], xt_sb[:, half * 512:(half + 1) * 512], ALPHA, pX, ALU.mult, ALU.add,
                    [bX] + b_xt, b_zt)
                P.op("dve", (lambda half: (lambda e: e.bn_stats(stats[:, half, :], zt_sb[:, half * 512:(half + 1) * 512])))(half),
                     b_zt, [b_stats])
            P.op("dve", lambda e: e.bn_aggr(mv[:, 0:2], stats[:, :, :]), [b_stats], [b_mv])
            ACT(mv[:, 2:3], mv[:, 1:2], AF.Sqrt, [b_mv, b_const], [b_mv], bias=eps_l[:], scale=1.0)
            RECIP(mv[:, 3:4], mv[:, 2:3], [b_mv], [b_mv])
            TS("dve", zt_sb, zt_sb, mv[:, 0:1], mv[:, 3:4], ALU.subtract, ALU.mult, b_zt + [b_mv], b_zt)
            TT("dve", zt_sb, zt_sb, lnp[:, 0:1024], ALU.mult, b_zt + all_Q, b_zt)
            TT("dve", zt_sb, zt_sb, lnp[:, 1024:2048], ALU.add, b_zt + all_Q, b_zt)
            if l == n_layers - 1:
                out_toks.append(DMA("sp", T["y"][tok0:tok0 + 128, :], zt_sb, b_zt, []))
            else:
                DMA("sp", T["x1o"].ap()[tok0:tok0 + 128, :], zt_sb, b_zt, [b_x1o[ti]])
                for hh in range(2):
                    zb, bzb = wk_b[hh], b_wkb[hh]
                    ACT(zb[:], zt_sb[:, hh * 512:(hh + 1) * 512], AF.Copy, b_zt, [bzb])
                    pX, bX = nxt_xy()
                    for k in range(4):
                        MM(pX[:, k * 128:(k + 1) * 128], zb[:, k * 128:(k + 1) * 128], ident[:], True, True, [bzb, b_const], [bX])
                    CP("dve", xT[:, hh * 4:(hh + 1) * 4, tok0:tok0 + 128], pX.rearrange("p (c n) -> p c n", c=4), [bX], [b_xT[blk]])

        def finish_block(blk):
            cols = slice(blk * 512, (blk + 1) * 512)
            if l < n_layers - 1:
                DMA("sp", T["x1s"][blk].ap().rearrange("(c p) n -> p c n", p=128), xT[:, :, cols], [b_xT[blk]], [b_x1s[blk]])
                groups = [[0, 1], [2, 3], [4, 5], [6, 7]]
                x1s_ap = T["x1s"][blk].ap().opt()
                G_ap = T["G"][blk].ap().opt()
                P.dma("pool", P.chan(), (lambda a, b: (lambda e: e.collective_compute("AllGather", ALU.bypass, replica_groups=groups,
                                                                                       ins=[a], outs=[b])))(x1s_ap, G_ap),
                      [b_x1s[blk]], [b_G[blk]], inc=1)

        for i0 in range(NWG - 1):
            load_wg(i0)
        gi = 0
        for blk in range(NBO):
            cols = slice(blk * 512, (blk + 1) * 512)
            merged, b_merged = mergedv[blk % 2], b_mergedv[blk % 2]
            for oc in range(8):
                mab = []
                for br in range(2):
                    wsb, bsrc = (w_ba_sb, b_V[0]) if br == 0 else (w_bb_sb, b_V[1])
                    ysrc, bys = (ya, b_ya) if br == 0 else (yb, b_yb)
                    wgt, bwgt = wg[gi % NWG], b_wg[gi % NWG]
                    if gi + NWG - 1 < len(items):
                        load_wg(gi + NWG - 1)
                    gi += 1
                    pX, bX = nxt_xy()
                    for c in range(4):
                        MM(pX, wsb[:, c, oc * 128:(oc + 1) * 128], ysrc[:, c, cols], c == 0, c == 3, bsrc + [bys[c][blk]], [bX])
                    pY, bY = nxt_xy()
                    for c in range(8):
                        MM(pY, wgt[:, c, :], xT[:, c, cols], c == 0, c == 7, bwgt + [b_xT[blk]], [bY])
                    tg, btg = wk_f[br], b_wkf[br]
                    ACT(tg[:], pY, AF.Tanh, [bY, b_sm], [btg], bias=sm[:, 2 + br * 8 + oc: 3 + br * 8 + oc], scale=0.5)
                    mm_, bmm = wk_f[2 + br], b_wkf[2 + br]
                    STT("dve", mm_[:], tg[:], 1.0, pX, ALU.add, ALU.mult, [btg, bX], [bmm])
                    mab.append((mm_, bmm))
                extra = (list(b_cn) if blk % 2 == 0 else list(b_cqn)) if first_use[blk % 2] else []
                first_use[blk % 2] = False
                TT("dve", merged[:, oc, :], mab[0][0][:], mab[1][0][:], ALU.add, [mab[0][1], mab[1][1]], b_merged + extra)
                if blk >= 1 and oc % 2 == 1:
                    emit_tile(blk - 1, oc // 2)
                    if oc == 7:
                        finish_block(blk - 1)
        for t in range(4):
            emit_tile(NBO - 1, t)
        finish_block(NBO - 1)

        if l < n_layers - 1:
            for blk in range(NBO):
                def dyn(e, blk=blk):
                    if blk == 0:
                        e.reg_load(xregs[0], sel_sb[0:1, 0:1])
                    src = bass.AP(T["G"][blk], xregs[0], [[512, 128], [128 * 512, 8], [1, 512]])
                    return e.dma_start(out=xT[:, :, SO + blk * 512: SO + (blk + 1) * 512], in_=src)
                P.dma("pool", P.ring_chan("pool"), dyn, [b_G[blk], b_sel], [b_xT[NBO + blk]])
        P.retire(b_mergedv[0] + b_mergedv[1], list(b_cn) + list(b_cqn))
    if DEBUG and n_layers > 1:
        out_toks.append(out_dbg)
    P.wait_tok("sp", out_toks)


def _tables(perm):
    pos = perm.astype(np.float32)
    C = np.ones((128, S), np.float32)
    Sn = np.zeros((128, S), np.float32)
    fd = (np.float32(ROPE_THETA) ** (-np.arange(8, dtype=np.float32) / np.float32(8))).astype(np.float32)
    fm = (np.float32(ROPE_THETA) ** (-np.arange(16, dtype=np.float32) / np.float32(16))).astype(np.float32)
    for r0 in (0, 64):
        for i in range(16):
            ang = (pos * fd[i % 8]).astype(np.float32)
            C[r0 + i] = np.cos(ang)
            Sn[r0 + i] = np.sin(ang)
    for i in range(32):
        ang = (pos * fm[i % 16]).astype(np.float32)
        C[32 + i] = np.cos(ang)
        Sn[32 + i] = np.sin(ang)
    return np.stack([C, Sn]).astype(np.float32)


def _pm():
    Pm = np.zeros((128, 128), np.float32)
    for r0 in (0, 64):
        for i in range(8):
            Pm[r0 + i + 8, r0 + i] = -1.0
            Pm[r0 + i, r0 + 8 + i] = 1.0
    for i in range(16):
        Pm[48 + i, 32 + i] = -1.0
        Pm[32 + i, 48 + i] = 1.0
    return Pm


def _layer_inputs(l, w_in, g_q, w_q_up, g_kv, w_kv_up, diff_lambda, g_diff, w_branch_a, w_branch_b, b_merge, w_out,
                  ln_gamma, ln_beta, slot=0):
    f = np.float32
    wq = np.asarray(w_q_up[l], f).reshape(256, 8, 96)
    wqp = np.concatenate([wq[:, :, 0:32], wq[:, :, 64:96], wq[:, :, 32:64]], axis=2).reshape(256, 768)
    wkv = np.asarray(w_kv_up[l], f).reshape(128, 8, 128)
    wkp = np.concatenate([wkv[:, :, 0:32], np.zeros((128, 8, 32), f), wkv[:, :, 32:64]], axis=2).reshape(128, 768)
    wvp = np.ascontiguousarray(wkv[:, :, 64:128]).reshape(128, 512)
    vecs = np.zeros((128, 20), f)
    vecs[:, 0] = np.asarray(g_kv[l], f)
    vecs[:, 1:3] = np.asarray(g_q[l], f).reshape(2, 128).T
    vecs[:, 3] = np.asarray(g_diff[l], f)
    vecs[:, 4:20] = np.asarray(b_merge[l], f).reshape(16, 128).T
    lp = np.asarray(diff_lambda[l], f)
    lam = np.broadcast_to(np.concatenate([lp[0], lp[2], lp[1], lp[3]])[None, :], (128, 256))
    lnp = np.broadcast_to(np.concatenate([np.asarray(ln_gamma[l], f), np.asarray(ln_beta[l], f)])[None, :], (128, 2 * D))
    sfx = str(slot)
    return {
        "w_in" + sfx: np.ascontiguousarray(np.asarray(w_in[l], f)[:, 0:416]),
        "wt" + sfx: np.ascontiguousarray(np.asarray(w_in[l], f)[:, 416:].reshape(8, 128, 36, 128).transpose(2, 1, 0, 3)).reshape(36, 128, 1024),
        "wqp" + sfx: np.ascontiguousarray(wqp), "wkp" + sfx: np.ascontiguousarray(wkp), "wvp" + sfx: wvp,
        "w_ba" + sfx: np.ascontiguousarray(np.asarray(w_branch_a[l], f)),
        "w_bb" + sfx: np.ascontiguousarray(np.asarray(w_branch_b[l], f)),
        "w_out" + sfx: np.ascontiguousarray(np.asarray(w_out[l], f)),
        "vecs" + sfx: vecs, "lam" + sfx: np.ascontiguousarray(lam), "lnp" + sfx: np.ascontiguousarray(lnp),
    }


def _run(layers, x_full, weights, perms, tabs, pmat):
    nc, T = build(len(layers))
    emit_program(nc, T, [0.8 - 0.6 * math.exp(-0.3 * l) for l in layers])
    lw = {}
    for slot, l in enumerate(layers):
        lw.update(_layer_inputs(l, slot=slot, **weights))
    ident = np.eye(128, dtype=np.float32)
    in_maps = []
    for c in range(8):
        b = c // 2
        xb = x_full[b][perms[c]]
        m = dict(lw)
        m["xT"] = np.ascontiguousarray(xb.T)
        m["xo"] = np.ascontiguousarray(xb[:SO])
        m["cs"] = tabs[c]
        m["pm"] = pmat
        m["ident"] = ident
        m["sel"] = np.array([[(1 - c % 2) * D * 512, 0]], np.int32)
        in_maps.append(m)
    res = run_bass_kernel_spmd(nc, in_maps, core_ids=list(range(8)))
    out = np.empty_like(x_full)
    for c in range(8):
        b = c // 2
        out[b][perms[c][:SO]] = res.results[c]["y"]
    if DEBUG:
        global LAST_RES
        LAST_RES = res
    return out


def kernel(x, w_in, g_q, w_q_up, g_kv, w_kv_up, diff_lambda, g_diff, w_branch_a, w_branch_b, b_merge, w_out,
           ln_gamma, ln_beta, n_layers=DEPTH):
    x = np.asarray(x, np.float32)
    weights = dict(w_in=w_in, g_q=g_q, w_q_up=w_q_up, g_kv=g_kv, w_kv_up=w_kv_up, diff_lambda=diff_lambda,
                   g_diff=g_diff, w_branch_a=w_branch_a, w_branch_b=w_branch_b, b_merge=b_merge, w_out=w_out,
                   ln_gamma=ln_gamma, ln_beta=ln_beta)
    perms = []
    for c in range(8):
        hf = c % 2
        own = np.arange(hf * SO, (hf + 1) * SO)
        oth = np.arange((1 - hf) * SO, (2 - hf) * SO)
        perms.append(np.concatenate([own, oth]))
    tabs = [_tables(p) for p in perms]
    pmat = _pm()
    return _run(list(range(n_layers)), x, weights, perms, tabs, pmat)
```

```python
import math
from contextlib import ExitStack

import numpy as np
import concourse.bass as bass
import concourse.mybir as mybir
from concourse.bass_utils import run_bass_kernel_spmd

F32 = mybir.dt.float32
BF16 = mybir.dt.bfloat16
AF = mybir.ActivationFunctionType
ALU = mybir.AluOpType

D = 1024
S = 4096
SO = 2048
NB = 8
NBO = 4
DEPTH = 2
IN_COLS = 5024
ALPHA = (2 * DEPTH) ** 0.25
LN_EPS = 1e-5
RMS_EPS = 1e-6
ROPE_THETA = 500000.0
MLA_SCALE = 1.0 / math.sqrt(96.0)
DIFF_SCALE = 1.0 / math.sqrt(64.0)
C_CQ, C_CKV, C_KR, C_GA, C_QD, C_KD, C_VD, C_GB, C_G = 0, 256, 384, 416, 928, 1440, 1952, 2464, 2976

ENGS = ("pe", "act", "dve", "pool", "sp")
DEBUG = False


class Buf:
    __slots__ = ("name", "w", "r")

    def __init__(self, name=""):
        self.name = name
        self.w = None
        self.r = []


class Chan:
    __slots__ = ("sem", "val")

    def __init__(self, sem):
        self.sem = sem
        self.val = 0


class Prog:
    def __init__(self, nc, es, same_engine_sync=True):
        self.nc = nc
        self.es = es
        self.same = same_engine_sync
        self.ops = {e: [] for e in ENGS}
        self.esem = {e: es.enter_context(nc.semaphore(f"s_{e}")) for e in ENGS if e != "sp"}
        self.nchan = 0
        self.rings = {}
        self.ring_i = {}

    def ring_chan(self, q, k=None):
        if q not in self.rings:
            n = {"pool": 12, "sp": 8}.get(q, 4)
            self.rings[q] = [self.chan() for _ in range(n)]
            self.ring_i[q] = 0
        r = self.rings[q]
        c = r[self.ring_i[q] % len(r)]
        self.ring_i[q] += 1
        return c

    def chan(self):
        self.nchan += 1
        return Chan(self.es.enter_context(self.nc.semaphore(f"ch{self.nchan}")))

    def sbuf(self, name, shape, dtype):
        return self.es.enter_context(self.nc.sbuf_tensor("sb_" + name, list(shape), dtype))

    def psum(self, name, shape, dtype):
        return self.es.enter_context(self.nc.psum_tensor("ps_" + name, list(shape), dtype))

    def _deps(self, eng, reads, writes):
        toks = []
        for b in reads:
            if b.w is not None:
                toks.append(b.w)
        for b in writes:
            if b.w is not None:
                toks.append(b.w)
            toks.extend(b.r)
        best = {}
        for t in toks:
            if t[0] == "eng" and t[1] == eng and (eng in ("pe", "sp") or not self.same):
                continue
            k = (t[0], t[1] if t[0] == "eng" else id(t[1]))
            if k not in best or best[k][2] < t[2]:
                best[k] = t
        return list(best.values())

    def _finish(self, tok, reads, writes):
        key = tok[1]
        for b in reads:
            b.r = [t for t in b.r if not (t[0] == tok[0] and (t[1] == key if tok[0] == "eng" else t[1] is key))]
            b.r.append(tok)
        for b in writes:
            b.w = tok
            b.r = []
        return tok

    def op(self, eng, fn, reads=(), writes=()):
        deps = self._deps(eng, reads, writes)
        idx = len(self.ops[eng])
        self.ops[eng].append({"fn": fn, "deps": deps, "inc": False, "dma": None})
        return self._finish(("eng", eng, idx), reads, writes)

    def dma(self, eng, chan, fn, reads=(), writes=(), inc=16):
        deps = self._deps(eng, reads, writes)
        if chan.val > 0:
            deps.append(("dma", chan, chan.val))
        chan.val += inc
        self.ops[eng].append({"fn": fn, "deps": deps, "inc": False, "dma": chan, "dinc": inc})
        return self._finish(("dma", chan, chan.val), reads, writes)

    def retire(self, alias_bufs, canon_bufs):
        for a in alias_bufs:
            toks = list(a.r)
            if a.w is not None:
                toks.append(a.w)
            for b in canon_bufs:
                b.r.extend(toks)
            a.w = None
            a.r = []

    def wait_tok(self, eng, toks):
        self.ops[eng].append({"fn": None, "deps": list(toks), "inc": False, "dma": None})

    def emit(self, block):
        for e in ENGS:
            for rec in self.ops[e]:
                for t in rec["deps"]:
                    if t[0] == "eng":
                        self.ops[t[1]][t[2]]["inc"] = True
        cnt = {}
        for e in ENGS:
            c = 0
            lst = []
            for rec in self.ops[e]:
                if rec["inc"]:
                    c += 1
                lst.append(c)
            cnt[e] = lst

        def run(e, engine):
            seen = {}
            for rec in self.ops[e]:
                need = {}
                for t in rec["deps"]:
                    if t[0] == "eng":
                        sem = self.esem[t[1]]
                        val = cnt[t[1]][t[2]]
                    else:
                        sem = t[1].sem
                        val = t[2]
                    k = id(sem)
                    if seen.get(k, 0) >= val:
                        continue
                    if k not in need or need[k][1] < val:
                        need[k] = (sem, val)
                for k, (sem, val) in need.items():
                    engine.wait_ge(sem, val)
                    seen[k] = val
                if rec["fn"] is None:
                    continue
                ins = rec["fn"](engine)
                if rec["dma"] is not None:
                    ins.then_inc(rec["dma"].sem, rec.get("dinc", 16))
                elif rec["inc"]:
                    ins.then_inc(self.esem[e], 1)

        @block.tensor
        def _(eng):
            run("pe", eng)

        @block.scalar
        def _(eng):
            run("act", eng)

        @block.vector
        def _(eng):
            run("dve", eng)

        @block.gpsimd
        def _(eng):
            run("pool", eng)

        @block.sync
        def _(eng):
            run("sp", eng)


def build(n_layers=1, final_out=True):
    nc = bass.Bass("TRN2", target_bir_lowering=False)
    dt_in = lambda name, shape: nc.dram_tensor(name, list(shape), F32, kind="ExternalInput").ap()
    xT_d = dt_in("xT", [D, S])
    xo_d = dt_in("xo", [SO, D])
    cs_d = dt_in("cs", [2, 128, S])
    pm_d = dt_in("pm", [128, 128])
    id_d = dt_in("ident", [128, 128])
    sel_d = nc.dram_tensor("sel", [1, 2], mybir.dt.int32, kind="ExternalInput").ap()
    x1o_t = nc.dram_tensor("x1o", [SO, D], F32)
    x1s_t = [nc.dram_tensor(f"x1s{i}", [D, 512], BF16) for i in range(NBO)]
    G_t = [nc.dram_tensor(f"G{i}", [2 * D, 512], BF16) for i in range(NBO)]
    L = []
    for l in range(n_layers):
        L.append(dict(
            w_in=dt_in(f"w_in{l}", [D, 416]),
            wt=dt_in(f"wt{l}", [36, 128, 1024]),
            wqp=dt_in(f"wqp{l}", [256, 768]),
            wkp=dt_in(f"wkp{l}", [128, 768]),
            wvp=dt_in(f"wvp{l}", [128, 512]),
            w_ba=dt_in(f"w_ba{l}", [512, D]),
            w_bb=dt_in(f"w_bb{l}", [512, D]),
            w_out=dt_in(f"w_out{l}", [D, D]),
            vecs=dt_in(f"vecs{l}", [128, 20]),
            lam=dt_in(f"lam{l}", [128, 256]),
            lnp=dt_in(f"lnp{l}", [128, 2 * D]),
            lam_init=0.8 - 0.6 * math.exp(-0.3 * l),
        ))
    y_d = nc.dram_tensor("y", [SO, D], F32, kind="ExternalOutput").ap()
    dbg_d = nc.dram_tensor("dbg", [D, S], BF16, kind="ExternalOutput").ap() if DEBUG else None
    return nc, dict(xT=xT_d, xo=xo_d, cs=cs_d, pm=pm_d, ident=id_d, sel=sel_d, x1o=x1o_t, x1s=x1s_t, G=G_t, L=L, y=y_d, dbg=dbg_d)


def emit_program(nc, T, lam_inits):
    with ExitStack() as es:
        P = Prog(nc, es)
        _emit(nc, P, T, lam_inits)
        with nc.Block() as block:
            P.emit(block)


def _emit(nc, P, T, lam_inits):
    def MM(out, lhsT, rhs, st, sp, R, W):
        return P.op("pe", lambda e: e.matmul(out, lhsT, rhs, start=st, stop=sp), R, W)

    def ACT(out, in_, func, R, W, bias=None, scale=None):
        kw = {}
        if bias is not None:
            kw["bias"] = bias
        if scale is not None:
            kw["scale"] = scale
        return P.op("act", lambda e: e.activation(out, in_, func, **kw), R, W)

    def TT(eng, out, in0, in1, op, R, W):
        return P.op(eng, lambda e: e.tensor_tensor(out, in0, in1, op), R, W)

    def TS(eng, out, in0, s1, s2, op0, op1, R, W):
        if op1 is None:
            return P.op(eng, lambda e: e.tensor_scalar(out, in0, s1, None, op0), R, W)
        return P.op(eng, lambda e: e.tensor_scalar(out, in0, s1, s2, op0, op1), R, W)

    def STT(eng, out, in0, sc, in1, op0, op1, R, W):
        return P.op(eng, lambda e: e.scalar_tensor_tensor(out, in0, sc, in1, op0, op1), R, W)

    def CP(eng, out, in_, R, W):
        return P.op(eng, lambda e: e.tensor_copy(out, in_), R, W)

    def RECIP(out, in_, R, W):
        return P.op("dve", lambda e: e.reciprocal(out, in_), R, W)

    def MEMSET(eng, ap, val, W):
        return P.op(eng, lambda e: e.memset(ap, val), (), W)

    def DMA(q, out, in_, R, W, ch=None):
        ch = ch or P.ring_chan(q)
        return P.dma(q, ch, lambda e: e.dma_start(out=out, in_=in_), R, W)

    xT = P.sbuf("xT", [128, 8, S], BF16)
    b_xT = [Buf(f"xT{i}") for i in range(NB)]
    Ct = P.sbuf("Ct", [128, S], BF16)
    St = P.sbuf("St", [128, S], BF16)
    b_cs = Buf("cs")
    pm = P.sbuf("pm", [128, 128], BF16)
    ones = P.sbuf("ones", [128, 128], BF16)
    b_const = Buf("const")
    eps_r = P.sbuf("eps_r", [128, 1], F32)
    eps_l = P.sbuf("eps_l", [128, 1], F32)
    arena1 = P.sbuf("arena1", [128, 8192], BF16)
    cn = arena1[:, 0:4096]
    cqn = arena1[:, 4096:8192]
    b_cn = [Buf(f"cn{i}") for i in range(NB)]
    b_cqn = [Buf(f"cqn{i}") for i in range(NBO)]
    KAB = P.sbuf("KAB", [128, 2, S], BF16)
    b_K = [[Buf(f"K{j}_{i}") for i in range(NB)] for j in range(2)]
    QAB = P.sbuf("QAB", [128, 2, SO], BF16)
    b_Q = [[Buf(f"Q{j}_{i}") for i in range(NBO)] for j in range(2)]
    VAB = P.sbuf("VAB", [128, 2, 32, 128], BF16)
    b_V = [[Buf(f"V{j}_{i}") for i in range(4)] for j in range(2)]
    ya = P.sbuf("ya", [128, 4, SO], BF16)
    yb = P.sbuf("yb", [128, 4, SO], BF16)
    b_ya = [[Buf() for _ in range(NBO)] for _ in range(4)]
    b_yb = [[Buf() for _ in range(NBO)] for _ in range(4)]
    NPT = 3
    Pt = [P.sbuf(f"Pt{i}", [128, 2, 512], BF16) for i in range(NPT)]
    b_Pt = [Buf(f"Pt{i}") for i in range(NPT)]
    w_lat = P.sbuf("w_lat", [128, 8, 416], BF16); b_wlat = Buf()
    wqp = P.sbuf("wqp", [128, 2, 768], BF16); b_wqp = Buf()
    wkp = P.sbuf("wkp", [128, 768], BF16); b_wkp = Buf()
    wvp = P.sbuf("wvp", [128, 512], BF16); b_wvp = Buf()
    w_lat_flat = w_lat[:, :, :].rearrange("p c n -> p (c n)")
    wga = [w_lat_flat[:, i * 1024:(i + 1) * 1024].rearrange("p (c n) -> p c n", c=8) for i in range(2)]
    b_wga = [Buf(), Buf()]
    sga = P.sbuf("sga", [128, 4, 512], F32)
    b_sga = [Buf() for _ in range(4)]
    vecs = P.sbuf("vecs", [128, 20], F32); b_vecs = Buf()
    sm = P.sbuf("sm", [128, 32], F32); b_sm = Buf()
    wk_f = [P.sbuf(f"wkf{i}", [128, 512], F32) for i in range(4)]
    b_wkf = [Buf() for _ in range(4)]
    wk_g = [w_lat_flat[:, 2048:3072].bitcast(F32), P.sbuf("wkg1", [128, 512], F32)]
    b_wkg = [Buf() for _ in range(2)]
    wk_b = [P.sbuf(f"wkb{i}", [128, 512], BF16) for i in range(2)]
    lam = wk_f[1][:, 0:256]; b_lam = b_wkf[1]
    b_wkb = [Buf() for _ in range(2)]

    stats = P.sbuf("stats", [128, 2, 6], F32); b_stats = Buf()
    mv = P.sbuf("mv", [128, 4], F32); b_mv = Buf()

    ps = P.psum("ps", [128, 8, 512], F32)
    b_ps = [Buf(f"ps{i}") for i in range(8)]
    SB = [(ps[:, 0:2, :], [b_ps[0], b_ps[1]]), (ps[:, 2:4, :], [b_ps[2], b_ps[3]])]
    ACC = [(ps[:, 4, :], b_ps[4]), (ps[:, 5, :], b_ps[5])]
    XY = [(ps[:, 6, :], b_ps[6]), (ps[:, 7, :], b_ps[7])]
    xy_i = [0]
    ALLB = [(ps[:, i, :], b_ps[i]) for i in (6, 7, 0, 1, 2, 3, 5)]
    ring = {"banks": XY}

    def nxt_bg():
        return XY[1]

    def nxt_xy():
        r = ring["banks"][xy_i[0] % len(ring["banks"])]
        xy_i[0] += 1
        return r

    MEMSET("dve", ones[:], 1.0, [b_const])
    MEMSET("dve", eps_r[:], RMS_EPS, [b_const])
    MEMSET("dve", eps_l[:], LN_EPS, [b_const])
    DMA("pool", pm[:], T["pm"], [], [b_const])
    ident = P.sbuf("ident", [128, 128], BF16)
    DMA("pool", ident[:], T["ident"], [], [b_const])
    sel_sb = P.sbuf("sel_sb", [1, 2], mybir.dt.int32); b_sel = Buf()
    DMA("sp", sel_sb[:], T["sel"], [], [b_sel])
    b_x1o = [Buf() for _ in range(16)]
    b_x1s = [Buf() for _ in range(NBO)]
    b_G = [Buf() for _ in range(NBO)]
    xregs = [nc.gpsimd.alloc_register(f"roff{i}") for i in range(1)]
    n_layers = len(T["L"])
    xT_src = T["xT"].rearrange("(c p) s -> p c s", p=128)
    DMA("pool", xT[:, :, 0:512], xT_src[:, :, 0:512], [], [b_xT[0]])

    def late_const_loads():
        DMA("pool", Ct[:], T["cs"][0], [], [b_cs])
        DMA("pool", St[:], T["cs"][1], [], [b_cs])
        for blk in range(1, NB):
            DMA("pool", xT[:, :, blk * 512:(blk + 1) * 512], xT_src[:, :, blk * 512:(blk + 1) * 512], [], [b_xT[blk]])

    for l, Lw in enumerate(T["L"]):
        lam_init = lam_inits[l]
        if DEBUG and l == 1:
            out_dbg = DMA("sp", T["dbg"].rearrange("(c p) s -> p c s", p=128), xT[:, :, :], list(b_xT), [])
        w_in_v = Lw["w_in"].rearrange("(c p) n -> p c n", p=128)
        wt_d = Lw["wt"]
        P.retire(b_wga + [b_wkg[0]], [b_wlat])
        DMA("pool", w_lat[:], w_in_v[:, :, 0:416], [], [b_wlat])
        DMA("sp", vecs[:], Lw["vecs"], [], [b_vecs])
        DMA("sp", lam[:], Lw["lam"], [], [b_lam])
        DMA("pool", wqp[:], Lw["wqp"].rearrange("(c p) n -> p c n", p=128), [], [b_wqp])
        DMA("pool", wkp[:], Lw["wkp"], [], [b_wkp])
        DMA("pool", wvp[:], Lw["wvp"], [], [b_wvp])
        if l == 0:
            late_const_loads()

        lprod = wk_f[0]
        TT("dve", lprod[:, 0:128], lam[:, 0:128], lam[:, 128:256], ALU.mult, [b_lam], [b_wkf[0]])
        P.op("dve", lambda e: e.tensor_reduce(sm[:, 20:22], lprod[:, 0:128].rearrange("p (a b) -> p a b", a=2),
                                              mybir.AxisListType.X, ALU.add), [b_wkf[0]], [b_sm])
        ACT(sm[:, 22:24], sm[:, 20:22], AF.Exp, [b_sm], [b_sm])
        TT("dve", sm[:, 24:25], sm[:, 23:24], sm[:, 22:23], ALU.subtract, [b_sm], [b_sm])
        TS("dve", sm[:, 0:1], sm[:, 24:25], -lam_init, None, ALU.add, None, [b_sm], [b_sm])
        TS("dve", sm[:, 1:2], vecs[:, 3:4], (1.0 - lam_init) * 0.5, None, ALU.mult, None, [b_vecs, b_sm], [b_sm])
        TS("dve", sm[:, 2:18], vecs[:, 4:20], 0.5, None, ALU.mult, None, [b_vecs, b_sm], [b_sm])

        ring["banks"] = ALLB
        def lat_block(blk):
            cols = slice(blk * 512, (blk + 1) * 512)
            pX, bX = nxt_xy()
            for c in range(8):
                MM(pX, w_lat[:, c, 256:384], xT[:, c, cols], c == 0, c == 7, [b_wlat, b_xT[blk]], [bX])
            sq, bsq = wk_b[0], b_wkb[0]
            ACT(sq[:], pX, AF.Square, [bX], [bsq])
            pY, bY = nxt_xy()
            MM(pY, ones[:], sq[:], True, True, [b_const, bsq], [bY])
            rs, brs = wk_f[1], b_wkf[1]
            ACT(rs[:], pY, AF.Sqrt, [bY, b_const], [brs], bias=eps_r[:], scale=1.0 / 128.0)
            RECIP(rs[:], rs[:], [brs], [brs])
            STT("dve", cn[:, cols], pX, vecs[:, 0:1], rs[:], ALU.mult, ALU.mult, [bX, b_vecs, brs], [b_cn[blk]])
            pX, bX = nxt_xy()
            for c in range(8):
                MM(pX[0:64, :], w_lat[:, c, 352:416], xT[:, c, cols], c == 0, c == 7, [b_wlat, b_xT[blk]], [bX])
            ab, bab = wk_b[1], b_wkb[1]
            CP("dve", ab[0:64, :], pX[0:64, :], [bX], [bab])
            pY, bY = nxt_xy()
            MM(pY[0:64, :], pm[0:64, 0:64], ab[0:64, :], True, True, [b_const, bab], [bY])
            t1, bt1 = wk_f[2], b_wkf[2]
            t2, bt2 = wk_f[3], b_wkf[3]
            TT("dve", t1[32:64, :], pX[32:64, :], Ct[32:64, cols], ALU.mult, [bX, b_cs], [bt1])
            TT("dve", t2[32:64, :], pY[32:64, :], St[32:64, cols], ALU.mult, [bY, b_cs], [bt2])
            TT("dve", KAB[32:64, 0, cols], t1[32:64, :], t2[32:64, :], ALU.add, [bt1, bt2], [b_K[0][blk]])
            TT("dve", KAB[32:64, 1, cols], t1[32:64, :], t2[32:64, :], ALU.add, [bt1, bt2], [b_K[1][blk]])

        def cq_block(blk):
            cols = slice(blk * 512, (blk + 1) * 512)
            pc = []
            for j in range(2):
                pX, bX = nxt_xy()
                for c in range(8):
                    MM(pX, w_lat[:, c, j * 128:(j + 1) * 128], xT[:, c, cols], c == 0, c == 7, [b_wlat, b_xT[blk]], [bX])
                pc.append((pX, bX))
            pZ, bZ = ACC[0]
            for j in range(2):
                sq, bsq = wk_b[j], b_wkb[j]
                ACT(sq[:], pc[j][0], AF.Square, [pc[j][1]], [bsq])
                MM(pZ, ones[:], sq[:], j == 0, j == 1, [b_const, bsq], [bZ])
            rs, brs = wk_f[1], b_wkf[1]
            ACT(rs[:], pZ, AF.Sqrt, [bZ, b_const], [brs], bias=eps_r[:], scale=1.0 / 256.0)
            RECIP(rs[:], rs[:], [brs], [brs])
            for j in range(2):
                STT("dve", cqn[:, j * 2048 + blk * 512: j * 2048 + (blk + 1) * 512], pc[j][0], vecs[:, 1 + j:2 + j], rs[:],
                    ALU.mult, ALU.mult, [pc[j][1], b_vecs, brs], [b_cqn[blk]])


        for blk in range(NBO):
            lat_block(blk)
        for blk in range(NBO):
            cq_block(blk)
        for blk in range(NBO, NB):
            lat_block(blk)
        ring["banks"] = XY
        for g in range(4):
            MEMSET("pool", VAB[:, 0, g * 8:(g + 1) * 8, 64:128], 1.0, [b_V[0][g]])
            MEMSET("pool", VAB[:, 1, g * 8:(g + 1) * 8, 0:64], 1.0, [b_V[1][g]])

        dWv = [arena1[:, 0:4096].rearrange("p (w c n) -> p w c n", w=4, c=8),
               arena1[:, 4096:8192].rearrange("p (w c n) -> p w c n", w=4, c=8)]
        b_dW = [b_cn, b_cqn]

        def mla_prod(h):
            j = h % 2
            Kv = KAB[:, j, :]
            Qv = QAB[:, j, :]
            Vv = VAB[:, j, :, :]
            for blk in range(NB):
                cols = slice(blk * 512, (blk + 1) * 512)
                pX, bX = nxt_xy()
                MM(pX[0:96, :], wkp[:, h * 96:(h + 1) * 96], cn[:, cols], True, True, [b_wkp, b_cn[blk]], [bX])
                CP("dve", Kv[0:32, cols], pX[0:32, :], [bX], [b_K[j][blk]])
                CP("dve", Kv[64:96, cols], pX[64:96, :], [bX], [b_K[j][blk]])
                yield
            voff = 0 if j == 0 else 64
            for g in range(4):
                pX, bX = nxt_xy()
                for i in range(8):
                    kt = g * 8 + i
                    MM(pX[:, i * 64:(i + 1) * 64], cn[:, kt * 128:(kt + 1) * 128], wvp[:, h * 64:(h + 1) * 64], True, True,
                       [b_cn[kt // 4], b_wvp], [bX])
                CP("dve", Vv[:, g * 8:(g + 1) * 8, voff:voff + 64], pX.rearrange("p (a b) -> p a b", a=8), [bX], [b_V[j][g]])
                yield
            for qb in range(NBO):
                cols = slice(qb * 512, (qb + 1) * 512)
                pX, bX = nxt_xy()
                for c in range(2):
                    MM(pX[0:96, :], wqp[:, c, h * 96:(h + 1) * 96], cqn[:, c * 2048 + qb * 512: c * 2048 + (qb + 1) * 512],
                       c == 0, c == 1, [b_wqp, b_cqn[qb]], [bX])
                CP("dve", Qv[0:96, cols], pX[0:96, :], [bX], [b_Q[j][qb]])
                t1, bt1 = wk_g[0], b_wkg[0]
                t2, bt2 = wk_g[1], b_wkg[1]
                TT("dve", t1[32:64, :], pX[32:64, :], Ct[32:64, cols], ALU.mult, [bX, b_cs], [bt1, b_wlat])
                yield
                pY, bY = nxt_xy()
                MM(pY[0:96, :], pm[0:96, 0:96], Qv[0:96, cols], True, True, [b_const, b_Q[j][qb]], [bY])
                TT("dve", t2[32:64, :], pY[32:64, :], St[32:64, cols], ALU.mult, [bY, b_cs], [bt2])
                TT("dve", Qv[32:64, cols], t1[32:64, :], t2[32:64, :], ALU.add, [bt1, bt2], [b_Q[j][qb]])
                yield

        def diff_prod(h):
            j = h % 2
            Kv = KAB[:, j, :]
            Qv = QAB[:, j, :]
            Vv = VAB[:, j, :, :]
            dW = dWv[j]
            bdW = b_dW[j]
            for w in range(4):
                DMA("pool", arena1[:, j * 4096 + w * 1024: j * 4096 + (w + 1) * 1024], wt_d[4 + w * 4 + h], [], bdW)
            yield

            def qk_prod(w, dst, bdst, nblk):
                for blk in range(nblk):
                    cols = slice(blk * 512, (blk + 1) * 512)
                    pX, bX = nxt_xy()
                    for c in range(8):
                        MM(pX, dW[:, w, c, :], xT[:, c, cols], c == 0, c == 7, bdW + [b_xT[blk]], [bX])
                    CP("dve", dst[:, cols], pX, [bX], [bdst[blk]])
                    t1, bt1 = wk_g[0], b_wkg[0]
                    t2, bt2 = wk_g[1], b_wkg[1]
                    for r0 in (0, 64):
                        TT("dve", t1[r0:r0 + 16, :], pX[r0:r0 + 16, :], Ct[r0:r0 + 16, cols], ALU.mult, [bX, b_cs], [bt1, b_wlat])
                    yield
                    pY, bY = nxt_xy()
                    MM(pY, pm[:], dst[:, cols], True, True, [b_const, bdst[blk]], [bY])
                    for r0 in (0, 64):
                        TT("dve", t2[r0:r0 + 16, :], pY[r0:r0 + 16, :], St[r0:r0 + 16, cols], ALU.mult, [bY, b_cs], [bt2])
                        TT("dve", dst[r0:r0 + 16, cols], t1[r0:r0 + 16, :], t2[r0:r0 + 16, :], ALU.add, [bt1, bt2], [bdst[blk]])
                    yield

            yield from qk_prod(1, Kv, b_K[j], NB)
            yield from qk_prod(0, Qv, b_Q[j], NBO)
            vtmp = wk_g[1][:, :].bitcast(BF16)[:, 0:512]
            bvt = b_wkg[1]
            for blk in range(NB):
                cols = slice(blk * 512, (blk + 1) * 512)
                pX, bX = nxt_xy()
                for c in range(8):
                    MM(pX, dW[:, 2, c, :], xT[:, c, cols], c == 0, c == 7, bdW + [b_xT[blk]], [bX])
                CP("dve", vtmp, pX, [bX], [bvt])
                yield
                pY, bY = nxt_xy()
                for i in range(4):
                    MM(pY[:, i * 128:(i + 1) * 128], vtmp[:, i * 128:(i + 1) * 128], ident[:], True, True, [bvt, b_const], [bY])
                CP("dve", Vv[:, blk * 4:(blk + 1) * 4, :], pY.rearrange("p (a b) -> p a b", a=4), [bY], [b_V[j][blk // 2]])
                yield

        def drain(gen):
            if gen is not None:
                for _ in gen:
                    pass

        def attention_pass(Kv, bK, kdim_lo, kdim_hi, Qv, bQ, pv_list, scale, bg=None, sum_acc=None):
            NP = 16

            def emit_S(kp):
                sb, bsb = SB[kp % 2]
                for i in range(2):
                    kt = 2 * kp + i
                    MM(sb[:, i, :], Kv[kdim_lo:kdim_hi, kt * 128:(kt + 1) * 128], Qv[kdim_lo:kdim_hi, :], True, True,
                       [bK[kt // 4], bQ], [bsb[i]])

            emit_S(0)
            emit_S(1)
            for kp in range(NP):
                sb, bsb = SB[kp % 2]
                pt, bpt = Pt[kp % NPT], b_Pt[kp % NPT]
                ACT(pt[:], sb, AF.Exp, bsb, [bpt], scale=scale)
                if kp + 2 < NP:
                    emit_S(kp + 2)
                for i in range(2):
                    kt = 2 * kp + i
                    for (lfn, acc, bacc) in pv_list:
                        lap, lb = lfn(kt)
                        MM(acc, lap, pt[:, i, :], kt == 0, kt == 31, lb + [bpt], [bacc])
                if sum_acc is not None:
                    pp, bpp = wk_b[kp % 2], b_wkb[kp % 2]
                    TT("dve", pp[:], pt[:, 0, :], pt[:, 1, :], ALU.add, [bpt], [bpp])
                    if kp >= 1:
                        MM(sum_acc[0], ones[:], wk_b[(kp - 1) % 2][:], kp == 1, False, [b_const, b_wkb[(kp - 1) % 2]], [sum_acc[1]])
                    if kp == NP - 1:
                        MM(sum_acc[0], ones[:], pp[:], False, True, [b_const, bpp], [sum_acc[1]])
                if bg is not None:
                    next(bg, None)

        drain(mla_prod(0))
        for h in range(8):
            j = h % 2
            pair = h // 2
            Kv = KAB[:, j, :]
            Qv = QAB[:, j, :]
            Vv = VAB[:, j, :, :]
            if j == 0:
                DMA("pool", w_lat_flat[:, (pair % 2) * 1024:(pair % 2 + 1) * 1024], wt_d[pair], [],
                    [b_wga[pair % 2]] + ([b_wlat] if pair < 2 else []))
            bg = mla_prod(h + 1) if h < 7 else diff_prod(0)
            for qb in range(NBO):
                cols = slice(qb * 512, (qb + 1) * 512)
                if j == 0:
                    pX, bX = nxt_xy()
                    for c in range(8):
                        MM(pX, wga[pair % 2][:, c, :], xT[:, c, cols], c == 0, c == 7, [b_wga[pair % 2], b_xT[qb]], [bX])
                    tg, btg = wk_f[0], b_wkf[0]
                    ACT(tg[:], pX, AF.Tanh, [bX], [btg], scale=0.5)
                    STT("dve", sga[:, qb, :], tg[:], 1.0, pX, ALU.add, ALU.mult, [btg, bX], [b_sga[qb]])
                acc, bacc = ACC[(h * 4 + qb) % 2]
                attention_pass(Kv, b_K[j], 0, 96, Qv[:, cols], b_Q[j][qb],
                               [(lambda kt, Vv=Vv, j=j: (Vv[:, kt, :], [b_V[j][kt // 8]]), acc, bacc)], MLA_SCALE, bg=bg)
                o_lo, s_lo = (0, 64) if j == 0 else (64, 0)
                rc, brc = wk_f[1], b_wkf[1]
                RECIP(rc[s_lo:s_lo + 64, :], acc[s_lo:s_lo + 64, :], [bacc], [brc])
                tt, btt = wk_f[2], b_wkf[2]
                TT("dve", tt[o_lo:o_lo + 64, :], acc[o_lo:o_lo + 64, :], rc[s_lo:s_lo + 64, :], ALU.mult, [bacc, brc], [btt])
                STT("dve", ya[o_lo:o_lo + 64, pair, cols], tt[o_lo:o_lo + 64, :], 0.5, sga[o_lo:o_lo + 64, qb, :], ALU.mult, ALU.mult,
                    [btt, b_sga[qb]], [b_ya[pair][qb]])
            drain(bg)

        for h in range(4):
            j = h % 2
            Kv = KAB[:, j, :]
            Qv = QAB[:, j, :]
            Vv = VAB[:, j, :, :]
            dW = dWv[j]
            bdW = b_dW[j]
            bg = diff_prod(h + 1) if h < 3 else None
            for qb in range(NBO):
                cols = slice(qb * 512, (qb + 1) * 512)
                tm = []
                for m in range(2):
                    acc, bacc = ACC[0]
                    sacc, bsacc = ACC[1]
                    attention_pass(Kv, b_K[j], m * 64, (m + 1) * 64, Qv[:, cols], b_Q[j][qb],
                                   [(lambda kt, Vv=Vv, j=j: (Vv[:, kt, :], [b_V[j][kt // 8]]), acc, bacc)],
                                   DIFF_SCALE, bg=bg, sum_acc=(sacc, bsacc))
                    rc, brc = wk_f[1], b_wkf[1]
                    t, bt = wk_f[2 + m], b_wkf[2 + m]
                    ACT(t[:], acc, AF.Copy, [bacc], [bt])
                    CP("dve", rc[:], sacc, [bsacc], [brc])
                    RECIP(rc[:], rc[:], [brc], [brc])
                    TT("dve", t[:], t[:], rc[:], ALU.mult, [bt, brc], [bt])
                    tm.append((t, bt))
                o, bo = wk_f[2], b_wkf[2]
                STT("dve", o[:], tm[1][0][:], sm[:, 0:1], tm[0][0][:], ALU.mult, ALU.add, [tm[1][1], tm[0][1], b_sm], [bo])
                sq, bsq = wk_b[0], b_wkb[0]
                ACT(sq[:], o[:], AF.Square, [bo], [bsq])
                pX, bX = nxt_xy()
                MM(pX, ones[:], sq[:], True, True, [b_const, bsq], [bX])
                rs, brs = wk_f[1], b_wkf[1]
                ACT(rs[:], pX, AF.Sqrt, [bX, b_const], [brs], bias=eps_l[:], scale=1.0 / 128.0)
                RECIP(rs[:], rs[:], [brs], [brs])
                pY, bY = nxt_xy()
                for c in range(8):
                    MM(pY, dW[:, 3, c, :], xT[:, c, cols], c == 0, c == 7, bdW + [b_xT[qb]], [bY])
                tg, btg = wk_f[0], b_wkf[0]
                ACT(tg[:], pY, AF.Tanh, [bY], [btg], scale=0.5)
                sg, bsg = wk_f[3], b_wkf[3]
                STT("dve", sg[:], tg[:], 1.0, pY, ALU.add, ALU.mult, [btg, bY], [bsg])
                STT("dve", o[:], o[:], sm[:, 1:2], rs[:], ALU.mult, ALU.mult, [bo, b_sm, brs], [bo])
                TT("dve", yb[:, h, cols], o[:], sg[:], ALU.mult, [bo, bsg], [b_yb[h][qb]])
            drain(bg)


        ring["banks"] = ALLB + [(ps[:, 4, :], b_ps[4])]
        all_K = [b for lst in b_K for b in lst]
        all_Q = [b for lst in b_Q for b in lst]
        all_V = [b for lst in b_V for b in lst]
        w_out_sb = KAB[:, :, :].rearrange("p a (c n) -> p (a c) n", n=1024)
        w_ba_sb = VAB[:, 0, :, :].rearrange("p (c a) n -> p c (a n)", c=4)
        w_bb_sb = VAB[:, 1, :, :].rearrange("p (c a) n -> p c (a n)", c=4)
        DMA("pool", w_out_sb, Lw["w_out"].rearrange("(c p) n -> p c n", p=128), [], all_K)
        TS("dve", w_out_sb, w_out_sb, 0.5, None, ALU.mult, None, all_K, all_K)
        DMA("pool", w_ba_sb, Lw["w_ba"].rearrange("(c p) n -> p c n", p=128), [], b_V[0])
        DMA("pool", w_bb_sb, Lw["w_bb"].rearrange("(c p) n -> p c n", p=128), [], b_V[1])
        lnp = QAB[:, :, :].rearrange("p a n -> p (a n)").bitcast(F32)
        DMA("sp", lnp, Lw["lnp"], [], all_Q)
        mergedv = [arena1[:, 0:4096].rearrange("p (c n) -> p c n", c=8), arena1[:, 4096:8192].rearrange("p (c n) -> p c n", c=8)]
        b_mergedv = [[Buf("merged0")], [Buf("merged1")]]
        first_use = {0: True, 1: True}
        NWG = NPT
        wg = [Pt[i][:, :, :].rearrange("p a n -> p (a n)").rearrange("p (c n) -> p c n", c=8) for i in range(NWG)]
        b_wg = [[b_Pt[i]] for i in range(NWG)]
        xt_sb = sga[:, 0:2, :].rearrange("p a n -> p (a n)")
        zt_sb = sga[:, 2:4, :].rearrange("p a n -> p (a n)")
        b_xt = b_sga[0:2]
        b_zt = b_sga[2:4]
        if l == 0:
            out_toks = []
        items = [(blk, oc, br) for blk in range(NBO) for oc in range(8) for br in range(2)]

        def load_wg(i):
            blk_, oc_, br_ = items[i]
            DMA("pool", Pt[i % NWG][:, :, :].rearrange("p a n -> p (a n)"), wt_d[20 + br_ * 8 + oc_], [], b_wg[i % NWG])

        def emit_tile(blk, t):
            merged, b_merged = mergedv[blk % 2], b_mergedv[blk % 2]
            tok0 = blk * 512 + t * 128
            ti = blk * 4 + t
            if l == 0:
                DMA("sp", xt_sb, T["xo"][tok0:tok0 + 128, :], [], b_xt)
            else:
                DMA("sp", xt_sb, T["x1o"].ap()[tok0:tok0 + 128, :], [b_x1o[ti]], b_xt)
            for half in range(2):
                pX, bX = nxt_xy()
                for c in range(8):
                    MM(pX, merged[:, c, t * 128:(t + 1) * 128], w_out_sb[:, c, half * 512:(half + 1) * 512], c == 0, c == 7,
                       b_merged + all_K, [bX])
                STT("dve", zt_sb[:, half * 512:(half + 1) * 512], xt_sb[:, half * 512:(half + 1) * 512], ALPHA, pX, ALU.mult, ALU.add,
                    [bX] + b_xt, b_zt)
                P.op("dve", (lambda half: (lambda e: e.bn_stats(stats[:, half, :], zt_sb[:, half * 512:(half + 1) * 512])))(half),
                     b_zt, [b_stats])
            P.op("dve", lambda e: e.bn_aggr(mv[:, 0:2], stats[:, :, :]), [b_stats], [b_mv])
            ACT(mv[:, 2:3], mv[:, 1:2], AF.Sqrt, [b_mv, b_const], [b_mv], bias=eps_l[:], scale=1.0)
            RECIP(mv[:, 3:4], mv[:, 2:3], [b_mv], [b_mv])
            TS("dve", zt_sb, zt_sb, mv[:, 0:1], mv[:, 3:4], ALU.subtract, ALU.mult, b_zt + [b_mv], b_zt)
            TT("dve", zt_sb, zt_sb, lnp[:, 0:1024], ALU.mult, b_zt + all_Q, b_zt)
            TT("dve", zt_sb, zt_sb, lnp[:, 1024:2048], ALU.add, b_zt + all_Q, b_zt)
            if l == n_layers - 1:
                out_toks.append(DMA("sp", T["y"][tok0:tok0 + 128, :], zt_sb, b_zt, []))
            else:
                DMA("sp", T["x1o"].ap()[tok0:tok0 + 128, :], zt_sb, b_zt, [b_x1o[ti]])
                for hh in range(2):
                    zb, bzb = wk_b[hh], b_wkb[hh]
                    ACT(zb[:], zt_sb[:, hh * 512:(hh + 1) * 512], AF.Copy, b_zt, [bzb])
                    pX, bX = nxt_xy()
                    for k in range(4):
                        MM(pX[:, k * 128:(k + 1) * 128], zb[:, k * 128:(k + 1) * 128], ident[:], True, True, [bzb, b_const], [bX])
                    CP("dve", xT[:, hh * 4:(hh + 1) * 4, tok0:tok0 + 128], pX.rearrange("p (c n) -> p c n", c=4), [bX], [b_xT[blk]])

        def finish_block(blk):
            cols = slice(blk * 512, (blk + 1) * 512)
            if l < n_layers - 1:
                DMA("sp", T["x1s"][blk].ap().rearrange("(c p) n -> p c n", p=128), xT[:, :, cols], [b_xT[blk]], [b_x1s[blk]])
                groups = [[0, 1], [2, 3], [4, 5], [6, 7]]
                x1s_ap = T["x1s"][blk].ap().opt()
                G_ap = T["G"][blk].ap().opt()
                P.dma("pool", P.chan(), (lambda a, b: (lambda e: e.collective_compute("AllGather", ALU.bypass, replica_groups=groups,
                                                                                       ins=[a], outs=[b])))(x1s_ap, G_ap),
                      [b_x1s[blk]], [b_G[blk]], inc=1)

        for i0 in range(NWG - 1):
            load_wg(i0)
        gi = 0
        for blk in range(NBO):
            cols = slice(blk * 512, (blk + 1) * 512)
            merged, b_merged = mergedv[blk % 2], b_mergedv[blk % 2]
            for oc in range(8):
                mab = []
                for br in range(2):
                    wsb, bsrc = (w_ba_sb, b_V[0]) if br == 0 else (w_bb_sb, b_V[1])
                    ysrc, bys = (ya, b_ya) if br == 0 else (yb, b_yb)
                    wgt, bwgt = wg[gi % NWG], b_wg[gi % NWG]
                    if gi + NWG - 1 < len(items):
                        load_wg(gi + NWG - 1)
                    gi += 1
                    pX, bX = nxt_xy()
                    for c in range(4):
                        MM(pX, wsb[:, c, oc * 128:(oc + 1) * 128], ysrc[:, c, cols], c == 0, c == 3, bsrc + [bys[c][blk]], [bX])
                    pY, bY = nxt_xy()
                    for c in range(8):
                        MM(pY, wgt[:, c, :], xT[:, c, cols], c == 0, c == 7, bwgt + [b_xT[blk]], [bY])
                    tg, btg = wk_f[br], b_wkf[br]
                    ACT(tg[:], pY, AF.Tanh, [bY, b_sm], [btg], bias=sm[:, 2 + br * 8 + oc: 3 + br * 8 + oc], scale=0.5)
                    mm_, bmm = wk_f[2 + br], b_wkf[2 + br]
                    STT("dve", mm_[:], tg[:], 1.0, pX, ALU.add, ALU.mult, [btg, bX], [bmm])
                    mab.append((mm_, bmm))
                extra = (list(b_cn) if blk % 2 == 0 else list(b_cqn)) if first_use[blk % 2] else []
                first_use[blk % 2] = False
                TT("dve", merged[:, oc, :], mab[0][0][:], mab[1][0][:], ALU.add, [mab[0][1], mab[1][1]], b_merged + extra)
                if blk >= 1 and oc % 2 == 1:
                    emit_tile(blk - 1, oc // 2)
                    if oc == 7:
                        finish_block(blk - 1)
        for t in range(4):
            emit_tile(NBO - 1, t)
        finish_block(NBO - 1)

        if l < n_layers - 1:
            for blk in range(NBO):
                def dyn(e, blk=blk):
                    if blk == 0:
                        e.reg_load(xregs[0], sel_sb[0:1, 0:1])
                    src = bass.AP(T["G"][blk], xregs[0], [[512, 128], [128 * 512, 8], [1, 512]])
                    return e.dma_start(out=xT[:, :, SO + blk * 512: SO + (blk + 1) * 512], in_=src)
                P.dma("pool", P.ring_chan("pool"), dyn, [b_G[blk], b_sel], [b_xT[NBO + blk]])
        P.retire(b_mergedv[0] + b_mergedv[1], list(b_cn) + list(b_cqn))
    if DEBUG and n_layers > 1:
        out_toks.append(out_dbg)
    P.wait_tok("sp", out_toks)


def _tables(perm):
    pos = perm.astype(np.float32)
    C = np.ones((128, S), np.float32)
    Sn = np.zeros((128, S), np.float32)
    fd = (np.float32(ROPE_THETA) ** (-np.arange(8, dtype=np.float32) / np.float32(8))).astype(np.float32)
    fm = (np.float32(ROPE_THETA) ** (-np.arange(16, dtype=np.float32) / np.float32(16))).astype(np.float32)
    for r0 in (0, 64):
        for i in range(16):
            ang = (pos * fd[i % 8]).astype(np.float32)
            C[r0 + i] = np.cos(ang)
            Sn[r0 + i] = np.sin(ang)
    for i in range(32):
        ang = (pos * fm[i % 16]).astype(np.float32)
        C[32 + i] = np.cos(ang)
        Sn[32 + i] = np.sin(ang)
    return np.stack([C, Sn]).astype(np.float32)


def _pm():
    Pm = np.zeros((128, 128), np.float32)
    for r0 in (0, 64):
        for i in range(8):
            Pm[r0 + i + 8, r0 + i] = -1.0
            Pm[r0 + i, r0 + 8 + i] = 1.0
    for i in range(16):
        Pm[48 + i, 32 + i] = -1.0
        Pm[32 + i, 48 + i] = 1.0
    return Pm


def _layer_inputs(l, w_in, g_q, w_q_up, g_kv, w_kv_up, diff_lambda, g_diff, w_branch_a, w_branch_b, b_merge, w_out,
                  ln_gamma, ln_beta, slot=0):
    f = np.float32
    wq = np.asarray(w_q_up[l], f).reshape(256, 8, 96)
    wqp = np.concatenate([wq[:, :, 0:32], wq[:, :, 64:96], wq[:, :, 32:64]], axis=2).reshape(256, 768)
    wkv = np.asarray(w_kv_up[l], f).reshape(128, 8, 128)
    wkp = np.concatenate([wkv[:, :, 0:32], np.zeros((128, 8, 32), f), wkv[:, :, 32:64]], axis=2).reshape(128, 768)
    wvp = np.ascontiguousarray(wkv[:, :, 64:128]).reshape(128, 512)
    vecs = np.zeros((128, 20), f)
    vecs[:, 0] = np.asarray(g_kv[l], f)
    vecs[:, 1:3] = np.asarray(g_q[l], f).reshape(2, 128).T
    vecs[:, 3] = np.asarray(g_diff[l], f)
    vecs[:, 4:20] = np.asarray(b_merge[l], f).reshape(16, 128).T
    lp = np.asarray(diff_lambda[l], f)
    lam = np.broadcast_to(np.concatenate([lp[0], lp[2], lp[1], lp[3]])[None, :], (128, 256))
    lnp = np.broadcast_to(np.concatenate([np.asarray(ln_gamma[l], f), np.asarray(ln_beta[l], f)])[None, :], (128, 2 * D))
    sfx = str(slot)
    return {
        "w_in" + sfx: np.ascontiguousarray(np.asarray(w_in[l], f)[:, 0:416]),
        "wt" + sfx: np.ascontiguousarray(np.asarray(w_in[l], f)[:, 416:].reshape(8, 128, 36, 128).transpose(2, 1, 0, 3)).reshape(36, 128, 1024),
        "wqp" + sfx: np.ascontiguousarray(wqp), "wkp" + sfx: np.ascontiguousarray(wkp), "wvp" + sfx: wvp,
        "w_ba" + sfx: np.ascontiguousarray(np.asarray(w_branch_a[l], f)),
        "w_bb" + sfx: np.ascontiguousarray(np.asarray(w_branch_b[l], f)),
        "w_out" + sfx: np.ascontiguousarray(np.asarray(w_out[l], f)),
        "vecs" + sfx: vecs, "lam" + sfx: np.ascontiguousarray(lam), "lnp" + sfx: np.ascontiguousarray(lnp),
    }


def _run(layers, x_full, weights, perms, tabs, pmat):
    nc, T = build(len(layers))
    emit_program(nc, T, [0.8 - 0.6 * math.exp(-0.3 * l) for l in layers])
    lw = {}
    for slot, l in enumerate(layers):
        lw.update(_layer_inputs(l, slot=slot, **weights))
    ident = np.eye(128, dtype=np.float32)
    in_maps = []
    for c in range(8):
        b = c // 2
        xb = x_full[b][perms[c]]
        m = dict(lw)
        m["xT"] = np.ascontiguousarray(xb.T)
        m["xo"] = np.ascontiguousarray(xb[:SO])
        m["cs"] = tabs[c]
        m["pm"] = pmat
        m["ident"] = ident
        m["sel"] = np.array([[(1 - c % 2) * D * 512, 0]], np.int32)
        in_maps.append(m)
    res = run_bass_kernel_spmd(nc, in_maps, core_ids=list(range(8)))
    out = np.empty_like(x_full)
    for c in range(8):
        b = c // 2
        out[b][perms[c][:SO]] = res.results[c]["y"]
    if DEBUG:
        global LAST_RES
        LAST_RES = res
    return out


def kernel(x, w_in, g_q, w_q_up, g_kv, w_kv_up, diff_lambda, g_diff, w_branch_a, w_branch_b, b_merge, w_out,
           ln_gamma, ln_beta, n_layers=DEPTH):
    x = np.asarray(x, np.float32)
    weights = dict(w_in=w_in, g_q=g_q, w_q_up=w_q_up, g_kv=g_kv, w_kv_up=w_kv_up, diff_lambda=diff_lambda,
                   g_diff=g_diff, w_branch_a=w_branch_a, w_branch_b=w_branch_b, b_merge=b_merge, w_out=w_out,
                   ln_gamma=ln_gamma, ln_beta=ln_beta)
    perms = []
    for c in range(8):
        hf = c % 2
        own = np.arange(hf * SO, (hf + 1) * SO)
        oth = np.arange((1 - hf) * SO, (2 - hf) * SO)
        perms.append(np.concatenate([own, oth]))
    tabs = [_tables(p) for p in perms]
    pmat = _pm()
    return _run(list(range(n_layers)), x, weights, perms, tabs, pmat)
```

```python
import math
from contextlib import ExitStack

import numpy as np
import concourse.bass as bass
import concourse.mybir as mybir
from concourse.bass_utils import run_bass_kernel_spmd

F32 = mybir.dt.float32
BF16 = mybir.dt.bfloat16
AF = mybir.ActivationFunctionType
ALU = mybir.AluOpType

D = 1024
S = 4096
SO = 2048
NB = 8
NBO = 4
DEPTH = 2
IN_COLS = 5024
ALPHA = (2 * DEPTH) ** 0.25
LN_EPS = 1e-5
RMS_EPS = 1e-6
ROPE_THETA = 500000.0
MLA_SCALE = 1.0 / math.sqrt(96.0)
DIFF_SCALE = 1.0 / math.sqrt(64.0)
C_CQ, C_CKV, C_KR, C_GA, C_QD, C_KD, C_VD, C_GB, C_G = 0, 256, 384, 416, 928, 1440, 1952, 2464, 2976

ENGS = ("pe", "act", "dve", "pool", "sp")
DEBUG = False


class Buf:
    __slots__ = ("name", "w", "r")

    def __init__(self, name=""):
        self.name = name
        self.w = None
        self.r = []


class Chan:
    __slots__ = ("sem", "val")

    def __init__(self, sem):
        self.sem = sem
        self.val = 0


class Prog:
    def __init__(self, nc, es, same_engine_sync=True):
        self.nc = nc
        self.es = es
        self.same = same_engine_sync
        self.ops = {e: [] for e in ENGS}
        self.esem = {e: es.enter_context(nc.semaphore(f"s_{e}")) for e in ENGS if e != "sp"}
        self.nchan = 0
        self.rings = {}
        self.ring_i = {}

    def ring_chan(self, q, k=None):
        if q not in self.rings:
            n = {"pool": 12, "sp": 8}.get(q, 4)
            self.rings[q] = [self.chan() for _ in range(n)]
            self.ring_i[q] = 0
        r = self.rings[q]
        c = r[self.ring_i[q] % len(r)]
        self.ring_i[q] += 1
        return c

    def chan(self):
        self.nchan += 1
        return Chan(self.es.enter_context(self.nc.semaphore(f"ch{self.nchan}")))

    def sbuf(self, name, shape, dtype):
        return self.es.enter_context(self.nc.sbuf_tensor("sb_" + name, list(shape), dtype))

    def psum(self, name, shape, dtype):
        return self.es.enter_context(self.nc.psum_tensor("ps_" + name, list(shape), dtype))

    def _deps(self, eng, reads, writes):
        toks = []
        for b in reads:
            if b.w is not None:
                toks.append(b.w)
        for b in writes:
            if b.w is not None:
                toks.append(b.w)
            toks.extend(b.r)
        best = {}
        for t in toks:
            if t[0] == "eng" and t[1] == eng and (eng in ("pe", "sp") or not self.same):
                continue
            k = (t[0], t[1] if t[0] == "eng" else id(t[1]))
            if k not in best or best[k][2] < t[2]:
                best[k] = t
        return list(best.values())

    def _finish(self, tok, reads, writes):
        key = tok[1]
        for b in reads:
            b.r = [t for t in b.r if not (t[0] == tok[0] and (t[1] == key if tok[0] == "eng" else t[1] is key))]
            b.r.append(tok)
        for b in writes:
            b.w = tok
            b.r = []
        return tok

    def op(self, eng, fn, reads=(), writes=()):
        deps = self._deps(eng, reads, writes)
        idx = len(self.ops[eng])
        self.ops[eng].append({"fn": fn, "deps": deps, "inc": False, "dma": None})
        return self._finish(("eng", eng, idx), reads, writes)

    def dma(self, eng, chan, fn, reads=(), writes=(), inc=16):
        deps = self._deps(eng, reads, writes)
        if chan.val > 0:
            deps.append(("dma", chan, chan.val))
        chan.val += inc
        self.ops[eng].append({"fn": fn, "deps": deps, "inc": False, "dma": chan, "dinc": inc})
        return self._finish(("dma", chan, chan.val), reads, writes)

    def retire(self, alias_bufs, canon_bufs):
        for a in alias_bufs:
            toks = list(a.r)
            if a.w is not None:
                toks.append(a.w)
            for b in canon_bufs:
                b.r.extend(toks)
            a.w = None
            a.r = []

    def wait_tok(self, eng, toks):
        self.ops[eng].append({"fn": None, "deps": list(toks), "inc": False, "dma": None})

    def emit(self, block):
        for e in ENGS:
            for rec in self.ops[e]:
                for t in rec["deps"]:
                    if t[0] == "eng":
                        self.ops[t[1]][t[2]]["inc"] = True
        cnt = {}
        for e in ENGS:
            c = 0
            lst = []
            for rec in self.ops[e]:
                if rec["inc"]:
                    c += 1
                lst.append(c)
            cnt[e] = lst

        def run(e, engine):
            seen = {}
            for rec in self.ops[e]:
                need = {}
                for t in rec["deps"]:
                    if t[0] == "eng":
                        sem = self.esem[t[1]]
                        val = cnt[t[1]][t[2]]
                    else:
                        sem = t[1].sem
                        val = t[2]
                    k = id(sem)
                    if seen.get(k, 0) >= val:
                        continue
                    if k not in need or need[k][1] < val:
                        need[k] = (sem, val)
                for k, (sem, val) in need.items():
                    engine.wait_ge(sem, val)
                    seen[k] = val
                if rec["fn"] is None:
                    continue
                ins = rec["fn"](engine)
                if rec["dma"] is not None:
                    ins.then_inc(rec["dma"].sem, rec.get("dinc", 16))
                elif rec["inc"]:
                    ins.then_inc(self.esem[e], 1)

        @block.tensor
        def _(eng):
            run("pe", eng)

        @block.scalar
        def _(eng):
            run("act", eng)

        @block.vector
        def _(eng):
            run("dve", eng)

        @block.gpsimd
        def _(eng):
            run("pool", eng)

        @block.sync
        def _(eng):
            run("sp", eng)


def build(n_layers=1, final_out=True):
    nc = bass.Bass("TRN2", target_bir_lowering=False)
    dt_in = lambda name, shape: nc.dram_tensor(name, list(shape), F32, kind="ExternalInput").ap()
    xT_d = dt_in("xT", [D, S])
    xo_d = dt_in("xo", [SO, D])
    cs_d = dt_in("cs", [2, 128, S])
    pm_d = dt_in("pm", [128, 128])
    id_d = dt_in("ident", [128, 128])
    sel_d = nc.dram_tensor("sel", [1, 2], mybir.dt.int32, kind="ExternalInput").ap()
    x1o_t = nc.dram_tensor("x1o", [SO, D], F32)
    x1s_t = [nc.dram_tensor(f"x1s{i}", [D, 512], BF16) for i in range(NBO)]
    G_t = [nc.dram_tensor(f"G{i}", [2 * D, 512], BF16) for i in range(NBO)]
    L = []
    for l in range(n_layers):
        L.append(dict(
            w_in=dt_in(f"w_in{l}", [D, 416]),
            wt=dt_in(f"wt{l}", [36, 128, 1024]),
            wqp=dt_in(f"wqp{l}", [256, 768]),
            wkp=dt_in(f"wkp{l}", [128, 768]),
            wvp=dt_in(f"wvp{l}", [128, 512]),
            w_ba=dt_in(f"w_ba{l}", [512, D]),
            w_bb=dt_in(f"w_bb{l}", [512, D]),
            w_out=dt_in(f"w_out{l}", [D, D]),
            vecs=dt_in(f"vecs{l}", [128, 20]),
            lam=dt_in(f"lam{l}", [128, 256]),
            lnp=dt_in(f"lnp{l}", [128, 2 * D]),
            lam_init=0.8 - 0.6 * math.exp(-0.3 * l),
        ))
    y_d = nc.dram_tensor("y", [SO, D], F32, kind="ExternalOutput").ap()
    dbg_d = nc.dram_tensor("dbg", [D, S], BF16, kind="ExternalOutput").ap() if DEBUG else None
    return nc, dict(xT=xT_d, xo=xo_d, cs=cs_d, pm=pm_d, ident=id_d, sel=sel_d, x1o=x1o_t, x1s=x1s_t, G=G_t, L=L, y=y_d, dbg=dbg_d)


def emit_program(nc, T, lam_inits):
    with ExitStack() as es:
        P = Prog(nc, es)
        _emit(nc, P, T, lam_inits)
        with nc.Block() as block:
            P.emit(block)


def _emit(nc, P, T, lam_inits):
    def MM(out, lhsT, rhs, st, sp, R, W):
        return P.op("pe", lambda e: e.matmul(out, lhsT, rhs, start=st, stop=sp), R, W)

    def ACT(out, in_, func, R, W, bias=None, scale=None):
        kw = {}
        if bias is not None:
            kw["bias"] = bias
        if scale is not None:
            kw["scale"] = scale
        return P.op("act", lambda e: e.activation(out, in_, func, **kw), R, W)

    def TT(eng, out, in0, in1, op, R, W):
        return P.op(eng, lambda e: e.tensor_tensor(out, in0, in1, op), R, W)

    def TS(eng, out, in0, s1, s2, op0, op1, R, W):
        if op1 is None:
            return P.op(eng, lambda e: e.tensor_scalar(out, in0, s1, None, op0), R, W)
        return P.op(eng, lambda e: e.tensor_scalar(out, in0, s1, s2, op0, op1), R, W)

    def STT(eng, out, in0, sc, in1, op0, op1, R, W):
        return P.op(eng, lambda e: e.scalar_tensor_tensor(out, in0, sc, in1, op0, op1), R, W)

    def CP(eng, out, in_, R, W):
        return P.op(eng, lambda e: e.tensor_copy(out, in_), R, W)

    def RECIP(out, in_, R, W):
        return P.op("dve", lambda e: e.reciprocal(out, in_), R, W)

    def MEMSET(eng, ap, val, W):
        return P.op(eng, lambda e: e.memset(ap, val), (), W)

    def DMA(q, out, in_, R, W, ch=None):
        ch = ch or P.ring_chan(q)
        return P.dma(q, ch, lambda e: e.dma_start(out=out, in_=in_), R, W)

    xT = P.sbuf("xT", [128, 8, S], BF16)
    b_xT = [Buf(f"xT{i}") for i in range(NB)]
    Ct = P.sbuf("Ct", [128, S], BF16)
    St = P.sbuf("St", [128, S], BF16)
    b_cs = Buf("cs")
    pm = P.sbuf("pm", [128, 128], BF16)
    ones = P.sbuf("ones", [128, 128], BF16)
    b_const = Buf("const")
    eps_r = P.sbuf("eps_r", [128, 1], F32)
    eps_l = P.sbuf("eps_l", [128, 1], F32)
    arena1 = P.sbuf("arena1", [128, 8192], BF16)
    cn = arena1[:, 0:4096]
    cqn = arena1[:, 4096:8192]
    b_cn = [Buf(f"cn{i}") for i in range(NB)]
    b_cqn = [Buf(f"cqn{i}") for i in range(NBO)]
    KAB = P.sbuf("KAB", [128, 2, S], BF16)
    b_K = [[Buf(f"K{j}_{i}") for i in range(NB)] for j in range(2)]
    QAB = P.sbuf("QAB", [128, 2, SO], BF16)
    b_Q = [[Buf(f"Q{j}_{i}") for i in range(NBO)] for j in range(2)]
    VAB = P.sbuf("VAB", [128, 2, 32, 128], BF16)
    b_V = [[Buf(f"V{j}_{i}") for i in range(4)] for j in range(2)]
    ya = P.sbuf("ya", [128, 4, SO], BF16)
    yb = P.sbuf("yb", [128, 4, SO], BF16)
    b_ya = [[Buf() for _ in range(NBO)] for _ in range(4)]
    b_yb = [[Buf() for _ in range(NBO)] for _ in range(4)]
    NPT = 3
    Pt = [P.sbuf(f"Pt{i}", [128, 2, 512], BF16) for i in range(NPT)]
    b_Pt = [Buf(f"Pt{i}") for i in range(NPT)]
    w_lat = P.sbuf("w_lat", [128, 8, 416], BF16); b_wlat = Buf()
    wqp = P.sbuf("wqp", [128, 2, 768], BF16); b_wqp = Buf()
    wkp = P.sbuf("wkp", [128, 768], BF16); b_wkp = Buf()
    wvp = P.sbuf("wvp", [128, 512], BF16); b_wvp = Buf()
    w_lat_flat = w_lat[:, :, :].rearrange("p c n -> p (c n)")
    wga = [w_lat_flat[:, i * 1024:(i + 1) * 1024].rearrange("p (c n) -> p c n", c=8) for i in range(2)]
    b_wga = [Buf(), Buf()]
    sga = P.sbuf("sga", [128, 4, 512], F32)
    b_sga = [Buf() for _ in range(4)]
    vecs = P.sbuf("vecs", [128, 20], F32); b_vecs = Buf()
    sm = P.sbuf("sm", [128, 32], F32); b_sm = Buf()
    wk_f = [P.sbuf(f"wkf{i}", [128, 512], F32) for i in range(4)]
    b_wkf = [Buf() for _ in range(4)]
    wk_g = [w_lat_flat[:, 2048:3072].bitcast(F32), P.sbuf("wkg1", [128, 512], F32)]
    b_wkg = [Buf() for _ in range(2)]
    wk_b = [P.sbuf(f"wkb{i}", [128, 512], BF16) for i in range(2)]
    lam = wk_f[1][:, 0:256]; b_lam = b_wkf[1]
    b_wkb = [Buf() for _ in range(2)]

    stats = P.sbuf("stats", [128, 2, 6], F32); b_stats = Buf()
    mv = P.sbuf("mv", [128, 4], F32); b_mv = Buf()

    ps = P.psum("ps", [128, 8, 512], F32)
    b_ps = [Buf(f"ps{i}") for i in range(8)]
    SB = [(ps[:, 0:2, :], [b_ps[0], b_ps[1]]), (ps[:, 2:4, :], [b_ps[2], b_ps[3]])]
    ACC = [(ps[:, 4, :], b_ps[4]), (ps[:, 5, :], b_ps[5])]
    XY = [(ps[:, 6, :], b_ps[6]), (ps[:, 7, :], b_ps[7])]
    xy_i = [0]
    ALLB = [(ps[:, i, :], b_ps[i]) for i in (6, 7, 0, 1, 2, 3, 5)]
    ring = {"banks": XY}

    def nxt_bg():
        return XY[1]

    def nxt_xy():
        r = ring["banks"][xy_i[0] % len(ring["banks"])]
        xy_i[0] += 1
        return r

    MEMSET("dve", ones[:], 1.0, [b_const])
    MEMSET("dve", eps_r[:], RMS_EPS, [b_const])
    MEMSET("dve", eps_l[:], LN_EPS, [b_const])
    DMA("pool", pm[:], T["pm"], [], [b_const])
    ident = P.sbuf("ident", [128, 128], BF16)
    DMA("pool", ident[:], T["ident"], [], [b_const])
    sel_sb = P.sbuf("sel_sb", [1, 2], mybir.dt.int32); b_sel = Buf()
    DMA("sp", sel_sb[:], T["sel"], [], [b_sel])
    b_x1o = [Buf() for _ in range(16)]
    b_x1s = [Buf() for _ in range(NBO)]
    b_G = [Buf() for _ in range(NBO)]
    xregs = [nc.gpsimd.alloc_register(f"roff{i}") for i in range(1)]
    n_layers = len(T["L"])
    xT_src = T["xT"].rearrange("(c p) s -> p c s", p=128)
    DMA("pool", xT[:, :, 0:512], xT_src[:, :, 0:512], [], [b_xT[0]])

    def late_const_loads():
        DMA("pool", Ct[:], T["cs"][0], [], [b_cs])
        DMA("pool", St[:], T["cs"][1], [], [b_cs])
        for blk in range(1, NB):
            DMA("pool", xT[:, :, blk * 512:(blk + 1) * 512], xT_src[:, :, blk * 512:(blk + 1) * 512], [], [b_xT[blk]])

    for l, Lw in enumerate(T["L"]):
        lam_init = lam_inits[l]
        if DEBUG and l == 1:
            out_dbg = DMA("sp", T["dbg"].rearrange("(c p) s -> p c s", p=128), xT[:, :, :], list(b_xT), [])
        w_in_v = Lw["w_in"].rearrange("(c p) n -> p c n", p=128)
        wt_d = Lw["wt"]
        P.retire(b_wga + [b_wkg[0]], [b_wlat])
        DMA("pool", w_lat[:], w_in_v[:, :, 0:416], [], [b_wlat])
        DMA("sp", vecs[:], Lw["vecs"], [], [b_vecs])
        DMA("sp", lam[:], Lw["lam"], [], [b_lam])
        DMA("pool", wqp[:], Lw["wqp"].rearrange("(c p) n -> p c n", p=128), [], [b_wqp])
        DMA("pool", wkp[:], Lw["wkp"], [], [b_wkp])
        DMA("pool", wvp[:], Lw["wvp"], [], [b_wvp])
        if l == 0:
            late_const_loads()

        lprod = wk_f[0]
        TT("dve", lprod[:, 0:128], lam[:, 0:128], lam[:, 128:256], ALU.mult, [b_lam], [b_wkf[0]])
        P.op("dve", lambda e: e.tensor_reduce(sm[:, 20:22], lprod[:, 0:128].rearrange("p (a b) -> p a b", a=2),
                                              mybir.AxisListType.X, ALU.add), [b_wkf[0]], [b_sm])
        ACT(sm[:, 22:24], sm[:, 20:22], AF.Exp, [b_sm], [b_sm])
        TT("dve", sm[:, 24:25], sm[:, 23:24], sm[:, 22:23], ALU.subtract, [b_sm], [b_sm])
        TS("dve", sm[:, 0:1], sm[:, 24:25], -lam_init, None, ALU.add, None, [b_sm], [b_sm])
        TS("dve", sm[:, 1:2], vecs[:, 3:4], (1.0 - lam_init) * 0.5, None, ALU.mult, None, [b_vecs, b_sm], [b_sm])
        TS("dve", sm[:, 2:18], vecs[:, 4:20], 0.5, None, ALU.mult, None, [b_vecs, b_sm], [b_sm])

        ring["banks"] = ALLB
        def lat_block(blk):
            cols = slice(blk * 512, (blk + 1) * 512)
            pX, bX = nxt_xy()
            for c in range(8):
                MM(pX, w_lat[:, c, 256:384], xT[:, c, cols], c == 0, c == 7, [b_wlat, b_xT[blk]], [bX])
            sq, bsq = wk_b[0], b_wkb[0]
            ACT(sq[:], pX, AF.Square, [bX], [bsq])
            pY, bY = nxt_xy()
            MM(pY, ones[:], sq[:], True, True, [b_const, bsq], [bY])
            rs, brs = wk_f[1], b_wkf[1]
            ACT(rs[:], pY, AF.Sqrt, [bY, b_const], [brs], bias=eps_r[:], scale=1.0 / 128.0)
            RECIP(rs[:], rs[:], [brs], [brs])
            STT("dve", cn[:, cols], pX, vecs[:, 0:1], rs[:], ALU.mult, ALU.mult, [bX, b_vecs, brs], [b_cn[blk]])
            pX, bX = nxt_xy()
            for c in range(8):
                MM(pX[0:64, :], w_lat[:, c, 352:416], xT[:, c, cols], c == 0, c == 7, [b_wlat, b_xT[blk]], [bX])
            ab, bab = wk_b[1], b_wkb[1]
            CP("dve", ab[0:64, :], pX[0:64, :], [bX], [bab])
            pY, bY = nxt_xy()
            MM(pY[0:64, :], pm[0:64, 0:64], ab[0:64, :], True, True, [b_const, bab], [bY])
            t1, bt1 = wk_f[2], b_wkf[2]
            t2, bt2 = wk_f[3], b_wkf[3]
            TT("dve", t1[32:64, :], pX[32:64, :], Ct[32:64, cols], ALU.mult, [bX, b_cs], [bt1])
            TT("dve", t2[32:64, :], pY[32:64, :], St[32:64, cols], ALU.mult, [bY, b_cs], [bt2])
            TT("dve", KAB[32:64, 0, cols], t1[32:64, :], t2[32:64, :], ALU.add, [bt1, bt2], [b_K[0][blk]])
            TT("dve", KAB[32:64, 1, cols], t1[32:64, :], t2[32:64, :], ALU.add, [bt1, bt2], [b_K[1][blk]])

        def cq_block(blk):
            cols = slice(blk * 512, (blk + 1) * 512)
            pc = []
            for j in range(2):
                pX, bX = nxt_xy()
                for c in range(8):
                    MM(pX, w_lat[:, c, j * 128:(j + 1) * 128], xT[:, c, cols], c == 0, c == 7, [b_wlat, b_xT[blk]], [bX])
                pc.append((pX, bX))
            pZ, bZ = ACC[0]
            for j in range(2):
                sq, bsq = wk_b[j], b_wkb[j]
                ACT(sq[:], pc[j][0], AF.Square, [pc[j][1]], [bsq])
                MM(pZ, ones[:], sq[:], j == 0, j == 1, [b_const, bsq], [bZ])
            rs, brs = wk_f[1], b_wkf[1]
            ACT(rs[:], pZ, AF.Sqrt, [bZ, b_const], [brs], bias=eps_r[:], scale=1.0 / 256.0)
            RECIP(rs[:], rs[:], [brs], [brs])
            for j in range(2):
                STT("dve", cqn[:, j * 2048 + blk * 512: j * 2048 + (blk + 1) * 512], pc[j][0], vecs[:, 1 + j:2 + j], rs[:],
                    ALU.mult, ALU.mult, [pc[j][1], b_vecs, brs], [b_cqn[blk]])


        for blk in range(NBO):
            lat_block(blk)
        for blk in range(NBO):
            cq_block(blk)
        for blk in range(NBO, NB):
            lat_block(blk)
        ring["banks"] = XY
        for g in range(4):
            MEMSET("pool", VAB[:, 0, g * 8:(g + 1) * 8, 64:128], 1.0, [b_V[0][g]])
            MEMSET("pool", VAB[:, 1, g * 8:(g + 1) * 8, 0:64], 1.0, [b_V[1][g]])

        dWv = [arena1[:, 0:4096].rearrange("p (w c n) -> p w c n", w=4, c=8),
               arena1[:, 4096:8192].rearrange("p (w c n) -> p w c n", w=4, c=8)]
        b_dW = [b_cn, b_cqn]

        def mla_prod(h):
            j = h % 2
            Kv = KAB[:, j, :]
            Qv = QAB[:, j, :]
            Vv = VAB[:, j, :, :]
            for blk in range(NB):
                cols = slice(blk * 512, (blk + 1) * 512)
                pX, bX = nxt_xy()
                MM(pX[0:96, :], wkp[:, h * 96:(h + 1) * 96], cn[:, cols], True, True, [b_wkp, b_cn[blk]], [bX])
                CP("dve", Kv[0:32, cols], pX[0:32, :], [bX], [b_K[j][blk]])
                CP("dve", Kv[64:96, cols], pX[64:96, :], [bX], [b_K[j][blk]])
                yield
            voff = 0 if j == 0 else 64
            for g in range(4):
                pX, bX = nxt_xy()
                for i in range(8):
                    kt = g * 8 + i
                    MM(pX[:, i * 64:(i + 1) * 64], cn[:, kt * 128:(kt + 1) * 128], wvp[:, h * 64:(h + 1) * 64], True, True,
                       [b_cn[kt // 4], b_wvp], [bX])
                CP("dve", Vv[:, g * 8:(g + 1) * 8, voff:voff + 64], pX.rearrange("p (a b) -> p a b", a=8), [bX], [b_V[j][g]])
                yield
            for qb in range(NBO):
                cols = slice(qb * 512, (qb + 1) * 512)
                pX, bX = nxt_xy()
                for c in range(2):
                    MM(pX[0:96, :], wqp[:, c, h * 96:(h + 1) * 96], cqn[:, c * 2048 + qb * 512: c * 2048 + (qb + 1) * 512],
                       c == 0, c == 1, [b_wqp, b_cqn[qb]], [bX])
                CP("dve", Qv[0:96, cols], pX[0:96, :], [bX], [b_Q[j][qb]])
                t1, bt1 = wk_g[0], b_wkg[0]
                t2, bt2 = wk_g[1], b_wkg[1]
                TT("dve", t1[32:64, :], pX[32:64, :], Ct[32:64, cols], ALU.mult, [bX, b_cs], [bt1, b_wlat])
                yield
                pY, bY = nxt_xy()
                MM(pY[0:96, :], pm[0:96, 0:96], Qv[0:96, cols], True, True, [b_const, b_Q[j][qb]], [bY])
                TT("dve", t2[32:64, :], pY[32:64, :], St[32:64, cols], ALU.mult, [bY, b_cs], [bt2])
                TT("dve", Qv[32:64, cols], t1[32:64, :], t2[32:64, :], ALU.add, [bt1, bt2], [b_Q[j][qb]])
                yield

        def diff_prod(h):
            j = h % 2
            Kv = KAB[:, j, :]
            Qv = QAB[:, j, :]
            Vv = VAB[:, j, :, :]
            dW = dWv[j]
            bdW = b_dW[j]
            for w in range(4):
                DMA("pool", arena1[:, j * 4096 + w * 1024: j * 4096 + (w + 1) * 1024], wt_d[4 + w * 4 + h], [], bdW)
            yield

            def qk_prod(w, dst, bdst, nblk):
                for blk in range(nblk):
                    cols = slice(blk * 512, (blk + 1) * 512)
                    pX, bX = nxt_xy()
                    for c in range(8):
                        MM(pX, dW[:, w, c, :], xT[:, c, cols], c == 0, c == 7, bdW + [b_xT[blk]], [bX])
                    CP("dve", dst[:, cols], pX, [bX], [bdst[blk]])
                    t1, bt1 = wk_g[0], b_wkg[0]
                    t2, bt2 = wk_g[1], b_wkg[1]
                    for r0 in (0, 64):
                        TT("dve", t1[r0:r0 + 16, :], pX[r0:r0 + 16, :], Ct[r0:r0 + 16, cols], ALU.mult, [bX, b_cs], [bt1, b_wlat])
                    yield
                    pY, bY = nxt_xy()
                    MM(pY, pm[:], dst[:, cols], True, True, [b_const, bdst[blk]], [bY])
                    for r0 in (0, 64):
                        TT("dve", t2[r0:r0 + 16, :], pY[r0:r0 + 16, :], St[r0:r0 + 16, cols], ALU.mult, [bY, b_cs], [bt2])
                        TT("dve", dst[r0:r0 + 16, cols], t1[r0:r0 + 16, :], t2[r0:r0 + 16, :], ALU.add, [bt1, bt2], [bdst[blk]])
                    yield

            yield from qk_prod(1, Kv, b_K[j], NB)
            yield from qk_prod(0, Qv, b_Q[j], NBO)
            for kt in range(32):
                pX, bX = nxt_xy()
                for c in range(8):
                    MM(pX[:, 0:128], xT[:, c, kt * 128:(kt + 1) * 128], dW[:, 2, c, :], c == 0, c == 7,
                       bdW + [b_xT[kt // 4]], [bX])
                CP("dve", Vv[:, kt, :], pX[:, 0:128], [bX], [b_V[j][kt // 8]])
                yield

        def drain(gen):
            if gen is not None:
                for _ in gen:
                    pass

        def attention_pass(Kv, bK, kdim_lo, kdim_hi, Qv, bQ, pv_list, scale, bg=None, sum_acc=None, pair_s=False):
            NP = 16

            def emit_S(kp):
                sb, bsb = SB[kp % 2]
                for i in range(2):
                    kt = 2 * kp + i
                    MM(sb[:, i, :], Kv[kdim_lo:kdim_hi, kt * 128:(kt + 1) * 128], Qv[kdim_lo:kdim_hi, :], True, True,
                       [bK[kt // 4], bQ], [bsb[i]])

            def do_exp(kp):
                sb, bsb = SB[kp % 2]
                pt, bpt = Pt[kp % NPT], b_Pt[kp % NPT]
                ACT(pt[:], sb, AF.Exp, bsb, [bpt], scale=scale)

            def do_pv(kp):
                pt, bpt = Pt[kp % NPT], b_Pt[kp % NPT]
                for i in range(2):
                    kt = 2 * kp + i
                    for (lfn, acc, bacc) in pv_list:
                        lap, lb = lfn(kt)
                        MM(acc, lap, pt[:, i, :], kt == 0, kt == 31, lb + [bpt], [bacc])
                if sum_acc is not None:
                    pp, bpp = wk_b[kp % 2], b_wkb[kp % 2]
                    TT("dve", pp[:], pt[:, 0, :], pt[:, 1, :], ALU.add, [bpt], [bpp])
                    if kp >= 1:
                        MM(sum_acc[0], ones[:], wk_b[(kp - 1) % 2][:], kp == 1, False, [b_const, b_wkb[(kp - 1) % 2]], [sum_acc[1]])
                    if kp == NP - 1:
                        MM(sum_acc[0], ones[:], pp[:], False, True, [b_const, bpp], [sum_acc[1]])
                if bg is not None:
                    next(bg, None)

            emit_S(0)
            emit_S(1)
            if pair_s:
                for kp in range(0, NP, 2):
                    do_exp(kp)
                    do_exp(kp + 1)
                    if kp + 2 < NP:
                        emit_S(kp + 2)
                        emit_S(kp + 3)
                    do_pv(kp)
                    do_pv(kp + 1)
            else:
                for kp in range(NP):
                    do_exp(kp)
                    if kp + 2 < NP:
                        emit_S(kp + 2)
                    do_pv(kp)

        drain(mla_prod(0))
        for h in range(8):
            j = h % 2
            pair = h // 2
            Kv = KAB[:, j, :]
            Qv = QAB[:, j, :]
            Vv = VAB[:, j, :, :]
            if j == 0:
                DMA("pool", w_lat_flat[:, (pair % 2) * 1024:(pair % 2 + 1) * 1024], wt_d[pair], [],
                    [b_wga[pair % 2]] + ([b_wlat] if pair < 2 else []))
            bg = mla_prod(h + 1) if h < 7 else diff_prod(0)
            for qb in range(NBO):
                cols = slice(qb * 512, (qb + 1) * 512)
                if j == 0:
                    pX, bX = nxt_xy()
                    for c in range(8):
                        MM(pX, wga[pair % 2][:, c, :], xT[:, c, cols], c == 0, c == 7, [b_wga[pair % 2], b_xT[qb]], [bX])
                    tg, btg = wk_f[0], b_wkf[0]
                    ACT(tg[:], pX, AF.Tanh, [bX], [btg], scale=0.5)
                    STT("dve", sga[:, qb, :], tg[:], 1.0, pX, ALU.add, ALU.mult, [btg, bX], [b_sga[qb]])
                acc, bacc = ACC[(h * 4 + qb) % 2]
                attention_pass(Kv, b_K[j], 0, 96, Qv[:, cols], b_Q[j][qb],
                               [(lambda kt, Vv=Vv, j=j: (Vv[:, kt, :], [b_V[j][kt // 8]]), acc, bacc)], MLA_SCALE, bg=bg)
                o_lo, s_lo = (0, 64) if j == 0 else (64, 0)
                rc, brc = wk_f[1], b_wkf[1]
                RECIP(rc[s_lo:s_lo + 64, :], acc[s_lo:s_lo + 64, :], [bacc], [brc])
                tt, btt = wk_f[2], b_wkf[2]
                TT("dve", tt[o_lo:o_lo + 64, :], acc[o_lo:o_lo + 64, :], rc[s_lo:s_lo + 64, :], ALU.mult, [bacc, brc], [btt])
                STT("dve", ya[o_lo:o_lo + 64, pair, cols], tt[o_lo:o_lo + 64, :], 0.5, sga[o_lo:o_lo + 64, qb, :], ALU.mult, ALU.mult,
                    [btt, b_sga[qb]], [b_ya[pair][qb]])
            drain(bg)

        for h in range(4):
            j = h % 2
            Kv = KAB[:, j, :]
            Qv = QAB[:, j, :]
            Vv = VAB[:, j, :, :]
            dW = dWv[j]
            bdW = b_dW[j]
            bg = diff_prod(h + 1) if h < 3 else None
            for qb in range(NBO):
                cols = slice(qb * 512, (qb + 1) * 512)
                tm = []
                for m in range(2):
                    acc, bacc = ACC[0]
                    sacc, bsacc = ACC[1]
                    attention_pass(Kv, b_K[j], m * 64, (m + 1) * 64, Qv[:, cols], b_Q[j][qb],
                                   [(lambda kt, Vv=Vv, j=j: (Vv[:, kt, :], [b_V[j][kt // 8]]), acc, bacc)],
                                   DIFF_SCALE, bg=bg, sum_acc=(sacc, bsacc), pair_s=True)
                    rc, brc = wk_f[1], b_wkf[1]
                    t, bt = wk_f[2 + m], b_wkf[2 + m]
                    ACT(t[:], acc, AF.Copy, [bacc], [bt])
                    CP("dve", rc[:], sacc, [bsacc], [brc])
                    RECIP(rc[:], rc[:], [brc], [brc])
                    TT("dve", t[:], t[:], rc[:], ALU.mult, [bt, brc], [bt])
                    tm.append((t, bt))
                o, bo = wk_f[2], b_wkf[2]
                STT("dve", o[:], tm[1][0][:], sm[:, 0:1], tm[0][0][:], ALU.mult, ALU.add, [tm[1][1], tm[0][1], b_sm], [bo])
                sq, bsq = wk_b[0], b_wkb[0]
                ACT(sq[:], o[:], AF.Square, [bo], [bsq])
                pX, bX = nxt_xy()
                MM(pX, ones[:], sq[:], True, True, [b_const, bsq], [bX])
                rs, brs = wk_f[1], b_wkf[1]
                ACT(rs[:], pX, AF.Sqrt, [bX, b_const], [brs], bias=eps_l[:], scale=1.0 / 128.0)
                RECIP(rs[:], rs[:], [brs], [brs])
                pY, bY = nxt_xy()
                for c in range(8):
                    MM(pY, dW[:, 3, c, :], xT[:, c, cols], c == 0, c == 7, bdW + [b_xT[qb]], [bY])
                tg, btg = wk_f[0], b_wkf[0]
                ACT(tg[:], pY, AF.Tanh, [bY], [btg], scale=0.5)
                sg, bsg = wk_f[3], b_wkf[3]
                STT("dve", sg[:], tg[:], 1.0, pY, ALU.add, ALU.mult, [btg, bY], [bsg])
                STT("dve", o[:], o[:], sm[:, 1:2], rs[:], ALU.mult, ALU.mult, [bo, b_sm, brs], [bo])
                TT("dve", yb[:, h, cols], o[:], sg[:], ALU.mult, [bo, bsg], [b_yb[h][qb]])
            drain(bg)


        ring["banks"] = ALLB + [(ps[:, 4, :], b_ps[4])]
        all_K = [b for lst in b_K for b in lst]
        all_Q = [b for lst in b_Q for b in lst]
        all_V = [b for lst in b_V for b in lst]
        w_out_sb = KAB[:, :, :].rearrange("p a (c n) -> p (a c) n", n=1024)
        w_ba_sb = VAB[:, 0, :, :].rearrange("p (c a) n -> p c (a n)", c=4)
        w_bb_sb = VAB[:, 1, :, :].rearrange("p (c a) n -> p c (a n)", c=4)
        DMA("pool", w_out_sb, Lw["w_out"].rearrange("(c p) n -> p c n", p=128), [], all_K)
        TS("dve", w_out_sb, w_out_sb, 0.5, None, ALU.mult, None, all_K, all_K)
        DMA("pool", w_ba_sb, Lw["w_ba"].rearrange("(c p) n -> p c n", p=128), [], b_V[0])
        DMA("pool", w_bb_sb, Lw["w_bb"].rearrange("(c p) n -> p c n", p=128), [], b_V[1])
        lnp = QAB[:, :, :].rearrange("p a n -> p (a n)").bitcast(F32)
        DMA("sp", lnp, Lw["lnp"], [], all_Q)
        mergedv = [arena1[:, 0:4096].rearrange("p (c n) -> p c n", c=8), arena1[:, 4096:8192].rearrange("p (c n) -> p c n", c=8)]
        b_mergedv = [[Buf("merged0")], [Buf("merged1")]]
        first_use = {0: True, 1: True}
        NWG = NPT
        wg = [Pt[i][:, :, :].rearrange("p a n -> p (a n)").rearrange("p (c n) -> p c n", c=8) for i in range(NWG)]
        b_wg = [[b_Pt[i]] for i in range(NWG)]
        xt_sb = sga[:, 0:2, :].rearrange("p a n -> p (a n)")
        zt_sb = sga[:, 2:4, :].rearrange("p a n -> p (a n)")
        b_xt = b_sga[0:2]
        b_zt = b_sga[2:4]
        if l == 0:
            out_toks = []
        items = [(blk, oc, br) for blk in range(NBO) for oc in range(8) for br in range(2)]

        def load_wg(i):
            blk_, oc_, br_ = items[i]
            DMA("pool", Pt[i % NWG][:, :, :].rearrange("p a n -> p (a n)"), wt_d[20 + br_ * 8 + oc_], [], b_wg[i % NWG])

        def emit_tile(blk, t):
            merged, b_merged = mergedv[blk % 2], b_mergedv[blk % 2]
            tok0 = blk * 512 + t * 128
            ti = blk * 4 + t
            if l == 0:
                DMA("sp", xt_sb, T["xo"][tok0:tok0 + 128, :], [], b_xt)
            else:
                DMA("sp", xt_sb, T["x1o"].ap()[tok0:tok0 + 128, :], [b_x1o[ti]], b_xt)
            for half in range(2):
                pX, bX = nxt_xy()
                for c in range(8):
                    MM(pX, merged[:, c, t * 128:(t + 1) * 128], w_out_sb[:, c, half * 512:(half + 1) * 512], c == 0, c == 7,
                       b_merged + all_K, [bX])
                STT("dve", zt_sb[:, half * 512:(half + 1) * 512], xt_sb[:, half * 512:(half + 1) * 512], ALPHA, pX, ALU.mult, ALU.add,
                    [bX] + b_xt, b_zt)
                P.op("dve", (lambda half: (lambda e: e.bn_stats(stats[:, half, :], zt_sb[:, half * 512:(half + 1) * 512])))(half),
                     b_zt, [b_stats])
            P.op("dve", lambda e: e.bn_aggr(mv[:, 0:2], stats[:, :, :]), [b_stats], [b_mv])
            ACT(mv[:, 2:3], mv[:, 1:2], AF.Sqrt, [b_mv, b_const], [b_mv], bias=eps_l[:], scale=1.0)
            RECIP(mv[:, 3:4], mv[:, 2:3], [b_mv], [b_mv])
            TS("dve", zt_sb, zt_sb, mv[:, 0:1], mv[:, 3:4], ALU.subtract, ALU.mult, b_zt + [b_mv], b_zt)
            TT("dve", zt_sb, zt_sb, lnp[:, 0:1024], ALU.mult, b_zt + all_Q, b_zt)
            TT("dve", zt_sb, zt_sb, lnp[:, 1024:2048], ALU.add, b_zt + all_Q, b_zt)
            if l == n_layers - 1:
                out_toks.append(DMA("sp", T["y"][tok0:tok0 + 128, :], zt_sb, b_zt, []))
            else:
                DMA("sp", T["x1o"].ap()[tok0:tok0 + 128, :], zt_sb, b_zt, [b_x1o[ti]])
                for hh in range(2):
                    zb, bzb = wk_b[hh], b_wkb[hh]
                    ACT(zb[:], zt_sb[:, hh * 512:(hh + 1) * 512], AF.Copy, b_zt, [bzb])
                    pX, bX = nxt_xy()
                    for k in range(4):
                        MM(pX[:, k * 128:(k + 1) * 128], zb[:, k * 128:(k + 1) * 128], ident[:], True, True, [bzb, b_const], [bX])
                    CP("dve", xT[:, hh * 4:(hh + 1) * 4, tok0:tok0 + 128], pX.rearrange("p (c n) -> p c n", c=4), [bX], [b_xT[blk]])

        def finish_block(blk):
            cols = slice(blk * 512, (blk + 1) * 512)
            if l < n_layers - 1:
                DMA("sp", T["x1s"][blk].ap().rearrange("(c p) n -> p c n", p=128), xT[:, :, cols], [b_xT[blk]], [b_x1s[blk]])
                groups = [[0, 1], [2, 3], [4, 5], [6, 7]]
                x1s_ap = T["x1s"][blk].ap().opt()
                G_ap = T["G"][blk].ap().opt()
                P.dma("pool", P.chan(), (lambda a, b: (lambda e: e.collective_compute("AllGather", ALU.bypass, replica_groups=groups,
                                                                                       ins=[a], outs=[b])))(x1s_ap, G_ap),
                      [b_x1s[blk]], [b_G[blk]], inc=1)

        for i0 in range(NWG - 1):
            load_wg(i0)
        gi = 0
        for blk in range(NBO):
            cols = slice(blk * 512, (blk + 1) * 512)
            merged, b_merged = mergedv[blk % 2], b_mergedv[blk % 2]
            for oc in range(8):
                mab = []
                for br in range(2):
                    wsb, bsrc = (w_ba_sb, b_V[0]) if br == 0 else (w_bb_sb, b_V[1])
                    ysrc, bys = (ya, b_ya) if br == 0 else (yb, b_yb)
                    wgt, bwgt = wg[gi % NWG], b_wg[gi % NWG]
                    if gi + NWG - 1 < len(items):
                        load_wg(gi + NWG - 1)
                    gi += 1
                    pX, bX = nxt_xy()
                    for c in range(4):
                        MM(pX, wsb[:, c, oc * 128:(oc + 1) * 128], ysrc[:, c, cols], c == 0, c == 3, bsrc + [bys[c][blk]], [bX])
                    pY, bY = nxt_xy()
                    for c in range(8):
                        MM(pY, wgt[:, c, :], xT[:, c, cols], c == 0, c == 7, bwgt + [b_xT[blk]], [bY])
                    tg, btg = wk_f[br], b_wkf[br]
                    ACT(tg[:], pY, AF.Tanh, [bY, b_sm], [btg], bias=sm[:, 2 + br * 8 + oc: 3 + br * 8 + oc], scale=0.5)
                    mm_, bmm = wk_f[2 + br], b_wkf[2 + br]
                    STT("dve", mm_[:], tg[:], 1.0, pX, ALU.add, ALU.mult, [btg, bX], [bmm])
                    mab.append((mm_, bmm))
                extra = (list(b_cn) if blk % 2 == 0 else list(b_cqn)) if first_use[blk % 2] else []
                first_use[blk % 2] = False
                TT("dve", merged[:, oc, :], mab[0][0][:], mab[1][0][:], ALU.add, [mab[0][1], mab[1][1]], b_merged + extra)
                if blk >= 1 and oc % 2 == 1:
                    emit_tile(blk - 1, oc // 2)
                    if oc == 7:
                        finish_block(blk - 1)
        for t in range(4):
            emit_tile(NBO - 1, t)
        finish_block(NBO - 1)

        if l < n_layers - 1:
            for blk in range(NBO):
                def dyn(e, blk=blk):
                    if blk == 0:
                        e.reg_load(xregs[0], sel_sb[0:1, 0:1])
                    src = bass.AP(T["G"][blk], xregs[0], [[512, 128], [128 * 512, 8], [1, 512]])
                    return e.dma_start(out=xT[:, :, SO + blk * 512: SO + (blk + 1) * 512], in_=src)
                P.dma("pool", P.ring_chan("pool"), dyn, [b_G[blk], b_sel], [b_xT[NBO + blk]])
        P.retire(b_mergedv[0] + b_mergedv[1], list(b_cn) + list(b_cqn))
    if DEBUG and n_layers > 1:
        out_toks.append(out_dbg)
    P.wait_tok("sp", out_toks)


def _tables(perm):
    pos = perm.astype(np.float32)
    C = np.ones((128, S), np.float32)
    Sn = np.zeros((128, S), np.float32)
    fd = (np.float32(ROPE_THETA) ** (-np.arange(8, dtype=np.float32) / np.float32(8))).astype(np.float32)
    fm = (np.float32(ROPE_THETA) ** (-np.arange(16, dtype=np.float32) / np.float32(16))).astype(np.float32)
    for r0 in (0, 64):
        for i in range(16):
            ang = (pos * fd[i % 8]).astype(np.float32)
            C[r0 + i] = np.cos(ang)
            Sn[r0 + i] = np.sin(ang)
    for i in range(32):
        ang = (pos * fm[i % 16]).astype(np.float32)
        C[32 + i] = np.cos(ang)
        Sn[32 + i] = np.sin(ang)
    return np.stack([C, Sn]).astype(np.float32)


def _pm():
    Pm = np.zeros((128, 128), np.float32)
    for r0 in (0, 64):
        for i in range(8):
            Pm[r0 + i + 8, r0 + i] = -1.0
            Pm[r0 + i, r0 + 8 + i] = 1.0
    for i in range(16):
        Pm[48 + i, 32 + i] = -1.0
        Pm[32 + i, 48 + i] = 1.0
    return Pm


def _layer_inputs(l, w_in, g_q, w_q_up, g_kv, w_kv_up, diff_lambda, g_diff, w_branch_a, w_branch_b, b_merge, w_out,
                  ln_gamma, ln_beta, slot=0):
    f = np.float32
    wq = np.asarray(w_q_up[l], f).reshape(256, 8, 96)
    wqp = np.concatenate([wq[:, :, 0:32], wq[:, :, 64:96], wq[:, :, 32:64]], axis=2).reshape(256, 768)
    wkv = np.asarray(w_kv_up[l], f).reshape(128, 8, 128)
    wkp = np.concatenate([wkv[:, :, 0:32], np.zeros((128, 8, 32), f), wkv[:, :, 32:64]], axis=2).reshape(128, 768)
    wvp = np.ascontiguousarray(wkv[:, :, 64:128]).reshape(128, 512)
    vecs = np.zeros((128, 20), f)
    vecs[:, 0] = np.asarray(g_kv[l], f)
    vecs[:, 1:3] = np.asarray(g_q[l], f).reshape(2, 128).T
    vecs[:, 3] = np.asarray(g_diff[l], f)
    vecs[:, 4:20] = np.asarray(b_merge[l], f).reshape(16, 128).T
    lp = np.asarray(diff_lambda[l], f)
    lam = np.broadcast_to(np.concatenate([lp[0], lp[2], lp[1], lp[3]])[None, :], (128, 256))
    lnp = np.broadcast_to(np.concatenate([np.asarray(ln_gamma[l], f), np.asarray(ln_beta[l], f)])[None, :], (128, 2 * D))
    sfx = str(slot)
    return {
        "w_in" + sfx: np.ascontiguousarray(np.asarray(w_in[l], f)[:, 0:416]),
        "wt" + sfx: np.ascontiguousarray(np.asarray(w_in[l], f)[:, 416:].reshape(8, 128, 36, 128).transpose(2, 1, 0, 3)).reshape(36, 128, 1024),
        "wqp" + sfx: np.ascontiguousarray(wqp), "wkp" + sfx: np.ascontiguousarray(wkp), "wvp" + sfx: wvp,
        "w_ba" + sfx: np.ascontiguousarray(np.asarray(w_branch_a[l], f)),
        "w_bb" + sfx: np.ascontiguousarray(np.asarray(w_branch_b[l], f)),
        "w_out" + sfx: np.ascontiguousarray(np.asarray(w_out[l], f)),
        "vecs" + sfx: vecs, "lam" + sfx: np.ascontiguousarray(lam), "lnp" + sfx: np.ascontiguousarray(lnp),
    }


def _run(layers, x_full, weights, perms, tabs, pmat):
    nc, T = build(len(layers))
    emit_program(nc, T, [0.8 - 0.6 * math.exp(-0.3 * l) for l in layers])
    lw = {}
    for slot, l in enumerate(layers):
        lw.update(_layer_inputs(l, slot=slot, **weights))
    ident = np.eye(128, dtype=np.float32)
    in_maps = []
    for c in range(8):
        b = c // 2
        xb = x_full[b][perms[c]]
        m = dict(lw)
        m["xT"] = np.ascontiguousarray(xb.T)
        m["xo"] = np.ascontiguousarray(xb[:SO])
        m["cs"] = tabs[c]
        m["pm"] = pmat
        m["ident"] = ident
        m["sel"] = np.array([[(1 - c % 2) * D * 512, 0]], np.int32)
        in_maps.append(m)
    res = run_bass_kernel_spmd(nc, in_maps, core_ids=list(range(8)))
    out = np.empty_like(x_full)
    for c in range(8):
        b = c // 2
        out[b][perms[c][:SO]] = res.results[c]["y"]
    if DEBUG:
        global LAST_RES
        LAST_RES = res
    return out


def kernel(x, w_in, g_q, w_q_up, g_kv, w_kv_up, diff_lambda, g_diff, w_branch_a, w_branch_b, b_merge, w_out,
           ln_gamma, ln_beta, n_layers=DEPTH):
    x = np.asarray(x, np.float32)
    weights = dict(w_in=w_in, g_q=g_q, w_q_up=w_q_up, g_kv=g_kv, w_kv_up=w_kv_up, diff_lambda=diff_lambda,
                   g_diff=g_diff, w_branch_a=w_branch_a, w_branch_b=w_branch_b, b_merge=b_merge, w_out=w_out,
                   ln_gamma=ln_gamma, ln_beta=ln_beta)
    perms = []
    for c in range(8):
        hf = c % 2
        own = np.arange(hf * SO, (hf + 1) * SO)
        oth = np.arange((1 - hf) * SO, (2 - hf) * SO)
        perms.append(np.concatenate([own, oth]))
    tabs = [_tables(p) for p in perms]
    pmat = _pm()
    return _run(list(range(n_layers)), x, weights, perms, tabs, pmat)
```
